# Optimizing a Trainium2 kernel written in Bass

```python
import jax
import jax.numpy as jnp
from jax import lax
import numpy as np

D_MODEL = 1024
BATCH = 8
SEQ = 2048
DEPTH = 4

N_META = 16
D_MIX = 2 * D_MODEL
W_GRP = D_MIX // 4
CONV_K = 4
D_FF = ((8 * D_MODEL // 3 + 127) // 128) * 128
EPS = 1e-6
CHUNK = 64
LEAD_PAD = CHUNK - N_META

LRU_HEAD_DIM = 64
LRU_HEADS = W_GRP // LRU_HEAD_DIM
LRU_C = 8.0

GDN_HEAD_DIM = 128
GDN_HEADS = W_GRP // GDN_HEAD_DIM

SSD_HEAD_DIM = 64
SSD_HEADS = W_GRP // SSD_HEAD_DIM
SSD_GROUPS = 2
SSD_STATE = 128

S5_GROUP_CH = 16
S5_GROUPS = W_GRP // S5_GROUP_CH
S5_STATE = 64

IN_SPLITS = (W_GRP, W_GRP, 3 * W_GRP, W_GRP, GDN_HEADS, GDN_HEADS,
             W_GRP, W_GRP + 2 * SSD_GROUPS * SSD_STATE, SSD_HEADS, W_GRP)
D_IN = sum(IN_SPLITS)

kernel_name = "hymba_style_parallel_hybrid_trunk"

F32 = jnp.float32


def rms_norm(x, g):
    xf = x.astype(F32)
    y = xf * lax.rsqrt(jnp.mean(xf * xf, axis=-1, keepdims=True) + EPS)
    return (y * g.astype(F32)).astype(x.dtype)


def l2_norm(x):
    return x * lax.rsqrt(jnp.sum(x * x, axis=-1, keepdims=True) + EPS)


def causal_dwconv(x, w):
    ch = x.shape[-1]
    return lax.conv_general_dilated(
        x, w[:, None, :].astype(x.dtype), window_strides=(1,),
        padding=[(w.shape[0] - 1, 0)], dimension_numbers=("NWC", "WIO", "NWC"),
        feature_group_count=ch)


def swiglu(x, w_gate, w_up, w_down):
    return (jax.nn.silu(x @ w_gate) * (x @ w_up)) @ w_down


def front_pad(t, n):
    return jnp.pad(t, [(0, 0), (n, 0)] + [(0, 0)] * (t.ndim - 2))


def _linear_combine(e1, e2):
    a1, b1 = e1
    a2, b2 = e2
    return a1 * a2, a2 * b1 + b2


def _complex_linear_combine(e1, e2):
    a1r, a1i, b1r, b1i = e1
    a2r, a2i, b2r, b2i = e2
    return (a1r * a2r - a1i * a2i, a1r * a2i + a1i * a2r,
            a2r * b1r - a2i * b1i + b2r, a2r * b1i + a2i * b1r + b2i)


def rglru_mixer(u_x, u_gate, conv_w, conv_b, w_a, b_a, w_i, b_i, lam, norm_g):
    bsz, t, _ = u_x.shape
    xc = (causal_dwconv(u_x, conv_w) + conv_b).astype(F32)
    xh = xc.reshape(bsz, t, LRU_HEADS, LRU_HEAD_DIM)
    r = jax.nn.sigmoid(jnp.einsum("btsi,sij->btsj", xh, w_a.astype(F32)).reshape(bsz, t, W_GRP) + b_a)
    ig = jax.nn.sigmoid(jnp.einsum("btsi,sij->btsj", xh, w_i.astype(F32)).reshape(bsz, t, W_GRP) + b_i)
    log_a = -LRU_C * r * jax.nn.softplus(-lam.astype(F32))
    a = jnp.exp(log_a)
    b = jnp.sqrt(-jnp.expm1(2.0 * log_a)) * (ig * xc)
    _, h = lax.associative_scan(_linear_combine, (a, b), axis=1)
    y = jax.nn.gelu(u_gate.astype(F32)) * h
    return rms_norm(y, norm_g)


def chunk_gated_delta_rule(q, k, v, beta, g):
    bsz, tp, nh, dk = q.shape
    dv = v.shape[-1]
    nc = tp // CHUNK

    def blk(z):
        z = z.reshape((bsz, nc, CHUNK) + z.shape[2:])
        return jnp.moveaxis(jnp.moveaxis(z, 1, 0), 2, 3)

    q, k, v, beta, g = (blk(z) for z in (q, k, v, beta, g))
    g = jnp.cumsum(g, axis=-1)
    incl = jnp.tril(jnp.ones((CHUNK, CHUNK), dtype=bool))
    strict = jnp.tril(jnp.ones((CHUNK, CHUNK), dtype=bool), -1)
    decay = jnp.exp(jnp.where(incl, g[..., :, None] - g[..., None, :], -jnp.inf))
    k_beta = k * beta[..., None]
    lmat = jnp.where(strict, jnp.einsum("nbhcd,nbhsd->nbhcs", k_beta, k) * decay, 0.0)
    eye = jnp.eye(CHUNK, dtype=F32)
    rhs = jnp.concatenate([v * beta[..., None], k_beta * jnp.exp(g)[..., None]], axis=-1)
    sol = lax.linalg.triangular_solve(eye + lmat, rhs, left_side=True, lower=True,
                                      unit_diagonal=True)
    u, w = sol[..., :dv], sol[..., dv:]
    attn = jnp.einsum("nbhcd,nbhsd->nbhcs", q, k) * decay
    q_dec = q * jnp.exp(g)[..., None]
    k_dec = k * jnp.exp(g[..., -1:] - g)[..., None]
    last = jnp.exp(g[..., -1])

    def step(s, inp):
        u_n, w_n, attn_n, q_n, k_n, last_n = inp
        v_new = u_n - jnp.einsum("bhcd,bhde->bhce", w_n, s)
        o_n = jnp.einsum("bhcd,bhde->bhce", q_n, s) + jnp.einsum("bhcs,bhse->bhce", attn_n, v_new)
        s = s * last_n[..., None, None] + jnp.einsum("bhcd,bhce->bhde", k_n, v_new)
        return s, o_n

    s0 = jnp.zeros((bsz, nh, dk, dv), F32)
    _, o = lax.scan(step, s0, (u, w, attn, q_dec, k_dec, last))
    return jnp.moveaxis(o, 0, 1).transpose(0, 1, 3, 2, 4).reshape(bsz, tp, nh, dv)


def gdn_mixer(u_qkv, u_z, u_beta, u_alpha, conv_w, a_log, dt_bias, norm_g):
    bsz, t, _ = u_qkv.shape
    qkv = jax.nn.silu(causal_dwconv(u_qkv, conv_w).astype(F32))
    q, k, v = jnp.split(qkv, 3, axis=-1)
    hs = (bsz, t, GDN_HEADS, GDN_HEAD_DIM)
    q = l2_norm(q.reshape(hs)) * (GDN_HEAD_DIM ** -0.5)
    k = l2_norm(k.reshape(hs))
    v = v.reshape(hs)
    beta = jax.nn.sigmoid(u_beta.astype(F32))
    g = -jnp.exp(a_log.astype(F32)) * jax.nn.softplus(u_alpha.astype(F32) + dt_bias)
    o = chunk_gated_delta_rule(*(front_pad(z, LEAD_PAD) for z in (q, k, v, beta, g)))[:, LEAD_PAD:]
    o = rms_norm(o, norm_g) * jax.nn.silu(u_z.astype(F32).reshape(hs))
    return o.reshape(bsz, t, W_GRP)


def ssd_chunked(x, a_dt, bm, cm):
    bsz, tp, nh, hd = x.shape
    nc = tp // CHUNK
    hpg = nh // SSD_GROUPS
    x = x.reshape(bsz, nc, CHUNK, SSD_GROUPS, hpg, hd)
    a = a_dt.reshape(bsz, nc, CHUNK, SSD_GROUPS, hpg).transpose(0, 1, 3, 4, 2)
    bm = bm.reshape(bsz, nc, CHUNK, SSD_GROUPS, SSD_STATE)
    cm = cm.reshape(bsz, nc, CHUNK, SSD_GROUPS, SSD_STATE)
    a_cum = jnp.cumsum(a, axis=-1)
    incl = jnp.tril(jnp.ones((CHUNK, CHUNK), dtype=bool))
    lmat = jnp.exp(jnp.where(incl, a_cum[..., :, None] - a_cum[..., None, :], -jnp.inf))
    cb = jnp.einsum("bclgn,bcsgn->bcgls", cm, bm)
    y_diag = jnp.einsum("bcgls,bcgels,bcsgep->bclgep", cb, lmat, x)
    decay_states = jnp.exp(a_cum[..., -1:] - a_cum)
    states = jnp.einsum("bclgn,bcgel,bclgep->bcgepn", bm, decay_states, x)
    chunk_decay = jnp.exp(a_cum[..., -1])

    def step(s, inp):
        st, dec = inp
        return s * dec[..., None, None] + st, s

    _, s_in = lax.scan(step, jnp.zeros_like(states[:, 0]),
                       (jnp.moveaxis(states, 1, 0), jnp.moveaxis(chunk_decay, 1, 0)))
    s_in = jnp.moveaxis(s_in, 0, 1)
    y_off = jnp.einsum("bclgn,bcgepn,bcgel->bclgep", cm, s_in, jnp.exp(a_cum))
    return (y_diag + y_off).reshape(bsz, tp, nh, hd)


def ssd_mixer(u_z, u_xbc, u_dt, conv_w, conv_b, a_log, dt_bias, d_skip, norm_g):
    bsz, t, _ = u_z.shape
    xbc = jax.nn.silu((causal_dwconv(u_xbc, conv_w) + conv_b).astype(F32))
    xs, bm, cm = jnp.split(xbc, [W_GRP, W_GRP + SSD_GROUPS * SSD_STATE], axis=-1)
    xs = xs.reshape(bsz, t, SSD_HEADS, SSD_HEAD_DIM)
    bm = bm.reshape(bsz, t, SSD_GROUPS, SSD_STATE)
    cm = cm.reshape(bsz, t, SSD_GROUPS, SSD_STATE)
    dt = jax.nn.softplus(u_dt.astype(F32) + dt_bias)
    a = -jnp.exp(a_log.astype(F32))
    y = ssd_chunked(*(front_pad(z, LEAD_PAD) for z in (xs * dt[..., None], dt * a, bm, cm)))[:, LEAD_PAD:]
    y = y + d_skip[:, None] * xs
    gs = (bsz, t, SSD_GROUPS, W_GRP // SSD_GROUPS)
    y = y.reshape(gs) * jax.nn.silu(u_z.astype(F32).reshape(gs))
    y = rms_norm(y, norm_g.reshape(SSD_GROUPS, W_GRP // SSD_GROUPS))
    return y.reshape(bsz, t, W_GRP)


def s5_mixer(u, a_re, a_im, log_dt, b_re, b_im, c_re, c_im, d_skip, w_glu, norm_g):
    bsz, t, _ = u.shape
    uf = u.astype(F32)
    ug = uf.reshape(bsz, t, S5_GROUPS, S5_GROUP_CH)
    lam_re = jnp.minimum(a_re.astype(F32), -1e-4)
    lam_im = a_im.astype(F32)
    dt = jnp.exp(log_dt.astype(F32))[:, None]
    mag = jnp.exp(dt * lam_re)
    ab_re = mag * jnp.cos(dt * lam_im)
    ab_im = mag * jnp.sin(dt * lam_im)
    den = lam_re * lam_re + lam_im * lam_im
    f_re = ((ab_re - 1.0) * lam_re + ab_im * lam_im) / den
    f_im = (ab_im * lam_re - (ab_re - 1.0) * lam_im) / den
    bb_re = f_re[..., None] * b_re - f_im[..., None] * b_im
    bb_im = f_re[..., None] * b_im + f_im[..., None] * b_re
    bu_re = jnp.einsum("btgi,gpi->tbgp", ug, bb_re)
    bu_im = jnp.einsum("btgi,gpi->tbgp", ug, bb_im)
    shp = (t, 1, S5_GROUPS, S5_STATE)
    _, _, s_re, s_im = lax.associative_scan(
        _complex_linear_combine,
        (jnp.broadcast_to(ab_re, shp), jnp.broadcast_to(ab_im, shp), bu_re, bu_im), axis=0)
    y = jnp.einsum("tbgp,gip->btgi", s_re, c_re) - jnp.einsum("tbgp,gip->btgi", s_im, c_im)
    y = y.reshape(bsz, t, W_GRP) + d_skip * uf
    y = jax.nn.gelu(y)
    y = y * jax.nn.sigmoid(y @ w_glu.astype(F32))
    return rms_norm(y, norm_g)


def hybrid_mixer(h, w_in, w_out, lru_p, gdn_p, ssd_p, s5_p):
    proj = h @ w_in
    (a_x, a_gate, b_qkv, b_z, b_beta, b_alpha, c_z, c_xbc, c_dt, d_u) = jnp.split(
        proj, np.cumsum(IN_SPLITS)[:-1].tolist(), axis=-1)
    y_a = rglru_mixer(a_x, a_gate, *lru_p)
    y_b = gdn_mixer(b_qkv, b_z, b_beta, b_alpha, *gdn_p)
    y_c = ssd_mixer(c_z, c_xbc, c_dt, *ssd_p)
    y_d = s5_mixer(d_u, *s5_p)
    y = jnp.concatenate([y_a, y_b, y_c, y_d], axis=-1).astype(h.dtype)
    return y @ w_out


def setup_inputs(seed: int = 0) -> dict:
    key = jax.random.key(seed)
    ks = iter(jax.random.split(key, 64))
    L = DEPTH

    def nrm(shape, scale):
        return jax.random.normal(next(ks), shape, F32) * scale

    def gain(shape):
        return 1.0 + 0.02 * jax.random.normal(next(ks), shape, F32)

    def unif(shape, lo, hi):
        return jax.random.uniform(next(ks), shape, F32, lo, hi)

    def dt_bias(shape):
        dt = jnp.exp(unif(shape, float(np.log(1e-3)), float(np.log(1e-1))))
        return dt + jnp.log(-jnp.expm1(-dt))

    a0 = unif((L, W_GRP), 0.9, 0.999) ** (1.0 / LRU_C)
    xbc_w = W_GRP + 2 * SSD_GROUPS * SSD_STATE
    return {
        "x": nrm((BATCH, SEQ, D_MODEL), 1.0),
        "meta_tokens": nrm((N_META, D_MODEL), 1.0),
        "ffn1_norm": gain((L, D_MODEL)),
        "ffn1_w_gate": nrm((L, D_MODEL, D_FF), D_MODEL ** -0.5),
        "ffn1_w_up": nrm((L, D_MODEL, D_FF), D_MODEL ** -0.5),
        "ffn1_w_down": nrm((L, D_FF, D_MODEL), D_FF ** -0.5),
        "mix_norm": gain((L, D_MODEL)),
        "w_in": nrm((L, D_MODEL, D_IN), D_MODEL ** -0.5),
        "w_out": nrm((L, D_MIX, D_MODEL), D_MIX ** -0.5),
        "lru_conv_w": nrm((L, CONV_K, W_GRP), CONV_K ** -0.5),
        "lru_conv_b": nrm((L, W_GRP), 0.02),
        "lru_w_a": nrm((L, LRU_HEADS, LRU_HEAD_DIM, LRU_HEAD_DIM), LRU_HEAD_DIM ** -0.5),
        "lru_b_a": nrm((L, W_GRP), 0.02),
        "lru_w_i": nrm((L, LRU_HEADS, LRU_HEAD_DIM, LRU_HEAD_DIM), LRU_HEAD_DIM ** -0.5),
        "lru_b_i": nrm((L, W_GRP), 0.02),
        "lru_lambda": jnp.log(a0) - jnp.log1p(-a0),
        "lru_norm": gain((L, W_GRP)),
        "gdn_conv_w": nrm((L, CONV_K, 3 * W_GRP), CONV_K ** -0.5),
        "gdn_a_log": jnp.log(unif((L, GDN_HEADS), 1.0, 16.0)),
        "gdn_dt_bias": dt_bias((L, GDN_HEADS)),
        "gdn_norm": gain((L, GDN_HEAD_DIM)),
        "ssd_conv_w": nrm((L, CONV_K, xbc_w), CONV_K ** -0.5),
        "ssd_conv_b": nrm((L, xbc_w), 0.02),
        "ssd_a_log": jnp.log(unif((L, SSD_HEADS), 1.0, 16.0)),
        "ssd_dt_bias": dt_bias((L, SSD_HEADS)),
        "ssd_d": gain((L, SSD_HEADS)),
        "ssd_norm": gain((L, W_GRP)),
        "s5_a_re": -0.5 + nrm((L, S5_GROUPS, S5_STATE), 0.01),
        "s5_a_im": jnp.pi * jnp.arange(S5_STATE, dtype=F32) + nrm((L, S5_GROUPS, S5_STATE), 0.01),
        "s5_log_dt": unif((L, S5_GROUPS), float(np.log(1e-3)), float(np.log(1e-1))),
        "s5_b_re": nrm((L, S5_GROUPS, S5_STATE, S5_GROUP_CH), (2 * S5_GROUP_CH) ** -0.5),
        "s5_b_im": nrm((L, S5_GROUPS, S5_STATE, S5_GROUP_CH), (2 * S5_GROUP_CH) ** -0.5),
        "s5_c_re": nrm((L, S5_GROUPS, S5_GROUP_CH, S5_STATE), (2 * S5_STATE) ** -0.5),
        "s5_c_im": nrm((L, S5_GROUPS, S5_GROUP_CH, S5_STATE), (2 * S5_STATE) ** -0.5),
        "s5_d": nrm((L, W_GRP), 1.0),
        "s5_w_glu": nrm((L, W_GRP, W_GRP), W_GRP ** -0.5),
        "s5_norm": gain((L, W_GRP)),
        "ffn2_norm": gain((L, D_MODEL)),
        "ffn2_w_gate": nrm((L, D_MODEL, D_FF), D_MODEL ** -0.5),
        "ffn2_w_up": nrm((L, D_MODEL, D_FF), D_MODEL ** -0.5),
        "ffn2_w_down": nrm((L, D_FF, D_MODEL), D_FF ** -0.5),
        "final_norm": gain((D_MODEL,)),
    }


def reference(x, meta_tokens, ffn1_norm, ffn1_w_gate, ffn1_w_up, ffn1_w_down, mix_norm, w_in, w_out,
              lru_conv_w, lru_conv_b, lru_w_a, lru_b_a, lru_w_i, lru_b_i, lru_lambda, lru_norm,
              gdn_conv_w, gdn_a_log, gdn_dt_bias, gdn_norm,
              ssd_conv_w, ssd_conv_b, ssd_a_log, ssd_dt_bias, ssd_d, ssd_norm,
              s5_a_re, s5_a_im, s5_log_dt, s5_b_re, s5_b_im, s5_c_re, s5_c_im, s5_d, s5_w_glu, s5_norm,
              ffn2_norm, ffn2_w_gate, ffn2_w_up, ffn2_w_down, final_norm):
    bsz = x.shape[0]
    meta = jnp.broadcast_to(meta_tokens.astype(x.dtype)[None], (bsz, N_META, D_MODEL))
    h = jnp.concatenate([meta, x], axis=1)
    for l in range(DEPTH):
        h = h + 0.5 * swiglu(rms_norm(h, ffn1_norm[l]), ffn1_w_gate[l], ffn1_w_up[l], ffn1_w_down[l])
        lru_p = (lru_conv_w[l], lru_conv_b[l], lru_w_a[l], lru_b_a[l], lru_w_i[l], lru_b_i[l],
                 lru_lambda[l], lru_norm[l])
        gdn_p = (gdn_conv_w[l], gdn_a_log[l], gdn_dt_bias[l], gdn_norm[l])
        ssd_p = (ssd_conv_w[l], ssd_conv_b[l], ssd_a_log[l], ssd_dt_bias[l], ssd_d[l], ssd_norm[l])
        s5_p = (s5_a_re[l], s5_a_im[l], s5_log_dt[l], s5_b_re[l], s5_b_im[l], s5_c_re[l], s5_c_im[l],
                s5_d[l], s5_w_glu[l], s5_norm[l])
        h = h + hybrid_mixer(rms_norm(h, mix_norm[l]), w_in[l], w_out[l], lru_p, gdn_p, ssd_p, s5_p)
        h = h + 0.5 * swiglu(rms_norm(h, ffn2_norm[l]), ffn2_w_gate[l], ffn2_w_up[l], ffn2_w_down[l])
    return rms_norm(h, final_norm)[:, N_META:]
```

```python
from contextlib import ExitStack
import math
import numpy as np
import concourse.bass as bass
import concourse.mybir as mybir
from concourse.ap import AP
from concourse.bass_utils import run_bass_kernel_spmd

F32 = mybir.dt.float32
BF16 = mybir.dt.bfloat16
AF = mybir.ActivationFunctionType
ALU = mybir.AluOpType
AX = mybir.AxisListType

D = 1024
DFF = 2816
T = 2064
NT = 17
DIN = 5136
EPS = 1e-6
NCORES = 8
NEG = -30000.0


def tile_n(i):
    return 16 if i == 0 else 128


def tile_t0(i):
    return 0 if i == 0 else 16 + (i - 1) * 128


class Prog:
    ENG = ("pe", "act", "dve", "pool", "sp")

    def __init__(self, nc, es):
        self.nc = nc
        self.es = es
        self.ops = {e: [] for e in self.ENG}
        self.cnt = {e: 0 for e in self.ENG}
        self.sems = {}
        self.dcnt = {}
        self.waited = {}
        self.last_w = {}
        self.readers = {}
        self.epoch = 0
        self.cur = {}
        for e in self.ENG:
            self.cur[e] = "E_" + e
            self.sems["E_" + e] = es.enter_context(nc.semaphore("E_" + e))

    def _deps(self, eng, reads, writes):
        deps = {}

        def add(tok):
            if tok is None:
                return
            s, v = tok
            if eng == "pe" and s.startswith("E_pe"):
                return
            if deps.get(s, 0) < v:
                deps[s] = v

        for k in reads:
            add(self.last_w.get(k))
        for k in writes:
            add(self.last_w.get(k))
            for tok in self.readers.get(k, ()):
                add(tok)
        waits = []
        for s, v in deps.items():
            if self.waited.get((eng, s), 0) < v:
                self.waited[(eng, s)] = v
                waits.append((self.sems[s], v))
        return waits

    def _record(self, tok, reads, writes):
        for k in writes:
            self.last_w[k] = tok
            self.readers[k] = []
        for k in reads:
            if k in writes:
                continue
            self.readers.setdefault(k, []).append(tok)

    capture = None

    def replay_merged(self, A, B):
        ia = ib = 0
        na, nb = len(A), len(B)
        while ia < na or ib < nb:
            if ib >= nb or (ia < na and ia * nb <= ib * na):
                kind, a, kw = A[ia]
                ia += 1
            else:
                kind, a, kw = B[ib]
                ib += 1
            if kind == "op":
                self.op(*a, **kw)
            else:
                self.dma(*a, **kw)

    def op(self, eng, name, reads, writes, *args, **kw):
        if self.capture is not None:
            self.capture.append(("op", (eng, name, reads, writes) + args, kw))
            return None
        pk = [k for k in reads if k.startswith("ps") and k[2:].isdigit()]
        if pk:
            writes = list(writes) + [k for k in pk if k not in writes]
        waits = self._deps(eng, reads, writes)
        self.cnt[eng] += 1
        tok = (self.cur[eng], self.cnt[eng])
        self.ops[eng].append((waits, name, args, kw, self.sems[self.cur[eng]], 1))
        self._record(tok, reads, writes)
        return tok

    def dma(self, q, out, in_, sem, reads=(), writes=(), **kw):
        if self.capture is not None:
            kw2 = dict(kw)
            kw2["reads"] = reads
            kw2["writes"] = writes
            self.capture.append(("dma", (q, out, in_, sem), kw2))
            return None
        if sem not in self.sems:
            self.sems[sem] = self.es.enter_context(self.nc.semaphore(sem))
            self.dcnt[sem] = 0
        waits = self._deps(q, reads, writes)
        prev = self.dcnt[sem]
        if prev > 0 and self.waited.get((q, sem), 0) < prev:
            self.waited[(q, sem)] = prev
            waits.append((self.sems[sem], prev))
        self.dcnt[sem] = prev + 16
        tok = (sem, prev + 16)
        kw = dict(kw)
        kw["out"] = out
        kw["in_"] = in_
        self.ops[q].append((waits, "dma_start", (), kw, self.sems[sem], 16))
        self._record(tok, reads, writes)
        return tok

    def barrier(self):
        toks = [(self.cur[e], self.cnt[e]) for e in self.ENG if self.cnt[e] > 0]
        toks += [(s, c) for s, c in self.dcnt.items() if c > 0]
        for e in self.ENG:
            waits = []
            for s, v in toks:
                if s == self.cur[e]:
                    continue
                if self.waited.get((e, s), 0) < v:
                    self.waited[(e, s)] = v
                    waits.append((self.sems[s], v))
            self.ops[e].append((waits, None, None, None, None, 0))
        self.last_w = {}
        self.readers = {}

    def new_epoch(self):
        self.barrier()
        self.epoch += 1
        for e in self.ENG:
            nm = f"E_{e}_{self.epoch}"
            self.sems[nm] = self.es.enter_context(self.nc.semaphore(nm))
            self.cur[e] = nm
            self.cnt[e] = 0

    def wait_sem(self, eng, sem):
        self.ops[eng].append(([(self.sems[sem], self.dcnt[sem])], None, None, None, None, 0))

    def emit(self):
        nc = self.nc
        with nc.Block() as block:
            def mk(name):
                lst = self.ops[name]

                def body(eng):
                    for waits, nm, args, kw, sem, inc in lst:
                        for s, v in waits:
                            eng.wait_ge(s, v)
                        if nm is None:
                            continue
                        ins = getattr(eng, nm)(*args, **kw)
                        ins.then_inc(sem, inc)
                return body

            block.tensor(mk("pe"))
            block.scalar(mk("act"))
            block.vector(mk("dve"))
            block.gpsimd(mk("pool"))
            block.sync(mk("sp"))


class Arena:
    def __init__(self, tens, words):
        self.t = tens
        self.words = words
        self.off = 0

    def reset(self):
        self.off = 0

    def get(self, shape, dt=F32):
        nel = 1
        for s in shape[1:]:
            nel *= s
        w = nel if dt == F32 else (nel + 1) // 2
        w = (w + 1) // 2 * 2
        v = self.t[:, self.off:self.off + w]
        self.off += w
        assert self.off <= self.words, (self.off, self.words)
        if dt == BF16:
            v = v.bitcast(BF16)[:, 0:nel]
        if len(shape) == 3:
            v = v.rearrange("p (a b) -> p a b", a=shape[1])
        elif len(shape) == 4:
            v = v.rearrange("p (a b c) -> p a b c", a=shape[1], b=shape[2])
        return v


def bc(ap, shape):
    return ap.to_broadcast(list(shape))


def build(nlayers=4, final=True, dbg=None, stop_after=None):
    nc = bass.Bass("TRN2", target_bir_lowering=False)
    es = ExitStack()
    P = Prog(nc, es)
    es.enter_context(nc.allow_non_contiguous_dma(reason="small strided parameter loads"))

    def dram(name, shape, kind="ExternalInput"):
        return nc.dram_tensor(name, list(shape), F32, kind=kind).ap()

    L = nlayers
    Xd = dram("x", [2048, D])
    METAd = dram("meta_tokens", [16, D])
    shapes = {
        "ffn1_norm": [L, D], "ffn1_w_gate": [L, D, DFF], "ffn1_w_up": [L, D, DFF], "ffn1_w_down": [L, DFF, D],
        "mix_norm": [L, D], "w_in": [L, D, DIN], "w_out": [L, 2048, D],
        "lru_conv_w": [L, 4, 512], "lru_conv_b": [L, 512], "lru_w_a": [L, 8, 64, 64], "lru_b_a": [L, 512],
        "lru_w_i": [L, 8, 64, 64], "lru_b_i": [L, 512], "lru_lambda": [L, 512], "lru_norm": [L, 512],
        "gdn_conv_w": [L, 4, 1536], "gdn_a_log": [L, 4], "gdn_dt_bias": [L, 4], "gdn_norm": [L, 128],
        "ssd_conv_w": [L, 4, 1024], "ssd_conv_b": [L, 1024], "ssd_a_log": [L, 8], "ssd_dt_bias": [L, 8],
        "ssd_d": [L, 8], "ssd_norm": [L, 512],
        "s5_a_re": [L, 32, 64], "s5_a_im": [L, 32, 64], "s5_log_dt": [L, 32],
        "s5_b_re": [L, 32, 64, 16], "s5_b_im": [L, 32, 64, 16], "s5_c_re": [L, 32, 16, 64], "s5_c_im": [L, 32, 16, 64],
        "s5_d": [L, 512], "s5_w_glu": [L, 512, 512], "s5_norm": [L, 512],
        "ffn2_norm": [L, D], "ffn2_w_gate": [L, D, DFF], "ffn2_w_up": [L, D, DFF], "ffn2_w_down": [L, DFF, D],
        "final_norm": [D],
    }
    W = {n: dram(n, s) for n, s in shapes.items()}
    OUTd = dram("out", [2048, D], kind="ExternalOutput")
    if dbg:
        DBGd = dram("dbg", [T, D], kind="ExternalOutput")

    def sb(name, shape, dt=F32):
        return es.enter_context(nc.sbuf_tensor(name, list(shape), dt))

    h = sb("h", [128, NT, D])
    xnT = sb("xnT", [128, 8, T], BF16)
    ident = sb("ident", [128, 128])
    ones = sb("ones", [128, 128])
    triu = sb("triu", [128, 128])
    mincT = sb("mincT", [128, 128])
    mstrT = sb("mstrT", [128, 128])
    pstr = sb("pstr", [128, 128])
    gmask = sb("gmask", [128, 8])
    psb = [es.enter_context(nc.psum_tensor(f"ps{i}", [128, 512], F32)) for i in range(8)]
    junk = sb("junk", [128, D], BF16)
    xs = sb("xs", [128, 2, D])
    ss = sb("ss", [128, NT])
    rstd = sb("rstd", [128, NT])
    prow = sb("prow", [128, 128])
    pc = sb("pc", [128, 128])
    WA_t = sb("WA", [128, 11520])
    MA_t = sb("MA", [128, 12288])
    WA = Arena(WA_t, 11520)
    MA = Arena(MA_t, 12288)

    def PE(name, r, w, *a, **kw):
        return P.op("pe", name, r, w, *a, **kw)

    def ACT(name, r, w, *a, **kw):
        return P.op("act", name, r, w, *a, **kw)

    def DVE(name, r, w, *a, **kw):
        return P.op("dve", name, r, w, *a, **kw)

    def POOL(name, r, w, *a, **kw):
        return P.op("pool", name, r, w, *a, **kw)

    POOL("memset", [], ["ident"], ident[:], 0.0)
    POOL("affine_select", ["ident"], ["ident"], out=ident[:], in_=ident[:], pattern=[[-1, 128]],
         compare_op=ALU.not_equal, fill=1.0, base=0, channel_multiplier=1)
    POOL("memset", [], ["ones"], ones[:], 1.0)
    POOL("memset", [], ["triu"], triu[:], 1.0)
    POOL("affine_select", ["triu"], ["triu"], out=triu[:], in_=triu[:], pattern=[[1, 128]],
         compare_op=ALU.is_ge, fill=0.0, base=0, channel_multiplier=-1)
    POOL("memset", [], ["mincT"], mincT[:], 0.0)
    POOL("affine_select", ["mincT"], ["mincT"], out=mincT[:], in_=mincT[:], pattern=[[1, 128]],
         compare_op=ALU.is_ge, fill=NEG, base=0, channel_multiplier=-1)
    POOL("memset", [], ["mstrT"], mstrT[:], 0.0)
    POOL("affine_select", ["mstrT"], ["mstrT"], out=mstrT[:], in_=mstrT[:], pattern=[[1, 128]],
         compare_op=ALU.is_gt, fill=NEG, base=0, channel_multiplier=-1)
    POOL("memset", [], ["pstr"], pstr[:], 0.0)
    POOL("affine_select", ["pstr"], ["pstr"], out=pstr[:], in_=pstr[:], pattern=[[-1, 128]],
         compare_op=ALU.is_gt, fill=-NEG, base=0, channel_multiplier=1)
    DVE("tensor_reduce", ["ident"], ["gmask"], out=gmask[:], in_=ident[:].rearrange("p (g i) -> p g i", g=8),
        axis=AX.X, op=ALU.add)
    POOL("memset", [], ["ss"], ss[:], 1.0)

    P.dma("sp", h[0:16, 0, :], METAd[:, :], "D_h0", writes=["h:0"])
    xv = Xd.rearrange("(i p) d -> p i d", p=128)
    for i in range(1, NT):
        P.dma("sp", h[:, i, :], xv[:, i - 1, :], f"D_h{i}", writes=[f"h:{i}"])

    def ps3(b, n):
        return psb[b][:, :].rearrange("p (c t) -> p c t", c=4)[:, :, 0:n]

    def load_cols(srcs):
        r0 = 0
        for s in srcs:
            r = s.shape[0]
            P.dma("sp", prow[r0:r0 + r, :], s, "D_prow", writes=["prow"])
            r0 += r
        assert r0 <= 128
        PE("transpose", ["prow", "ident"], ["ps7"], out=psb[7][:, 0:r0], in_=prow[0:r0, :], identity=ident[0:r0, 0:r0])
        DVE("tensor_copy", ["ps7"], ["pc"], out=pc[:, 0:r0], in_=psb[7][:, 0:r0])

    def rsqrt_inplace(ap, key, scale, eps):
        DVE("tensor_scalar", [key], [key], out=ap, in0=ap, scalar1=scale, scalar2=eps, op0=ALU.mult, op1=ALU.add)
        ACT("activation", [key], [key], out=ap, in_=ap, func=AF.Ln)
        ACT("activation", [key], [key], out=ap, in_=ap, func=AF.Exp, scale=-0.5)

    def norm_phase(gvec):
        load_cols([gvec.rearrange("(k p) -> k p", p=128)])
        for i in range(NT):
            n = tile_n(i)
            ACT("activation", [f"h:{i}"], ["junk", "ss"], out=junk[:n, :], in_=h[:n, i, :], func=AF.Square,
                accum_out=ss[:n, i:i + 1])
        rsqrt_inplace(ss[:, :], "ss", 1.0 / D, EPS)
        for i in range(NT):
            n = tile_n(i)
            t0 = tile_t0(i)
            s = i % 2
            ACT("activation", [f"h:{i}", "ss"], [f"xs:{s}"], out=xs[:n, s, :], in_=h[:n, i, :], func=AF.Copy,
                scale=ss[:n, i:i + 1])
            for half in range(2):
                b = (i * 2 + half) % 4
                for q in range(4):
                    kc = half * 4 + q
                    PE("transpose", [f"xs:{s}", "ident"], [f"ps{b}"], out=psb[b][:, q * 128:q * 128 + n],
                       in_=xs[:n, s, kc * 128:(kc + 1) * 128], identity=ident[:n, :n])
                DVE("tensor_tensor", [f"ps{b}", "pc"], [f"xnT:{i}"], out=xnT[:, half * 4:half * 4 + 4, t0:t0 + n],
                    in0=ps3(b, n), in1=bc(pc[:, half * 4:half * 4 + 4].unsqueeze(2), [128, 4, n]), op=ALU.mult)

    FS = 256
    NS = DFF // FS
    blocks = [[0], [1, 2, 3, 4], [5, 6, 7, 8], [9, 10, 11, 12], [13, 14, 15, 16]]
    st = {"ffn_it": 0}

    def ffn_alloc(wgate, wup, wdown):
        WA.reset()
        wg = [WA.get([128, 8, FS], BF16) for _ in range(2)]
        wu = [WA.get([128, 8, FS], BF16) for _ in range(2)]
        wd = [WA.get([128, 2, D], BF16) for _ in range(2)]
        sg = [WA.get([128, 2, 512]) for _ in range(2)]
        actT = [WA.get([128, 2, 512], BF16) for _ in range(2)]
        wgv = wgate.rearrange("(k p) f -> p k f", p=128)
        wuv = wup.rearrange("(k p) f -> p k f", p=128)
        wdv = wdown.rearrange("(c p) d -> p c d", p=128)

        def load(sl):
            j = sl % 2
            P.dma("pool", wg[j], wgv[:, :, sl * FS:(sl + 1) * FS], f"D_wg{j}", writes=[f"wg{j}"])
            P.dma("pool", wu[j], wuv[:, :, sl * FS:(sl + 1) * FS], f"D_wu{j}", writes=[f"wu{j}"])
            P.dma("pool", wd[j], wdv[:, sl * 2:sl * 2 + 2, :], f"D_wd{j}", writes=[f"wd{j}"])

        load(0)
        st["ffn"] = (wg, wu, wd, sg, actT, load)

    def ffn_phase(wgate, wup, wdown):
        wg, wu, wd, sg, actT, load = st["ffn"]
        for sl in range(NS):
            j = sl % 2
            if sl + 1 < NS:
                load(sl + 1)
            for blk in blocks:
                a = st["ffn_it"] % 2
                st["ffn_it"] += 1
                t0 = tile_t0(blk[0])
                nt = sum(tile_n(i) for i in blk)
                rk = [f"xnT:{i}" for i in blk]
                for (wsb, wkey, pbase) in ((wg[j], f"wg{j}", 0), (wu[j], f"wu{j}", 2)):
                    for fc in range(2):
                        for kc in range(8):
                            PE("matmul", [wkey] + rk, [f"ps{pbase + fc}"], psb[pbase + fc][:, 0:nt],
                               lhsT=wsb[:, kc, fc * 128:(fc + 1) * 128], rhs=xnT[:, kc, t0:t0 + nt],
                               start=(kc == 0), stop=(kc == 7))
                for fc in range(2):
                    ACT("activation", [f"ps{fc}"], [f"sg{a}:{fc}"], out=sg[a][:, fc, 0:nt], in_=psb[fc][:, 0:nt], func=AF.Silu)
                    DVE("tensor_tensor", [f"sg{a}:{fc}", f"ps{2 + fc}"], [f"actT{a}:{fc}"], out=actT[a][:, fc, 0:nt],
                        in0=sg[a][:, fc, 0:nt], in1=psb[2 + fc][:, 0:nt], op=ALU.mult)
                off = 0
                for ti, i in enumerate(blk):
                    n = tile_n(i)
                    for dh in range(2):
                        pbi = 4 + ((ti * 2 + dh) % 4)
                        for fc in range(2):
                            PE("matmul", [f"actT{a}:{fc}", f"wd{j}"], [f"ps{pbi}"], psb[pbi][:n, :],
                               lhsT=actT[a][:, fc, off:off + n], rhs=wd[j][:, fc, dh * 512:(dh + 1) * 512],
                               start=(fc == 0), stop=(fc == 1))
                        DVE("scalar_tensor_tensor", [f"ps{pbi}", f"h:{i}"], [f"h:{i}"], out=h[:n, i, dh * 512:(dh + 1) * 512],
                            in0=psb[pbi][:n, :], scalar=0.5, in1=h[:n, i, dh * 512:(dh + 1) * 512], op0=ALU.mult, op1=ALU.add)
                    off += n

    def load_win(wbig, l, c0, c1):
        wv = W["w_in"][l].rearrange("(k p) c -> p k c", p=128)
        c = c0
        while c < c1:
            ce = min(c + 1024, c1)
            P.dma("pool", wbig[:, :, c - c0:ce - c0], wv[:, :, c:ce], "D_win", writes=["wbig"])
            c = ce

    def load_wout(wo, l, r0):
        wv = W["w_out"][l].rearrange("(k p) d -> p k d", p=128)
        P.dma("pool", wo, wv[:, r0 // 128:r0 // 128 + 4, :], "D_wo", writes=["wo"])

    def inproj_fm(i, wbig, col0, nch, banks):
        n, t0 = tile_n(i), tile_t0(i)
        for c in range(nch):
            b = banks[c // 4]
            for kc in range(8):
                PE("matmul", ["wbig", f"xnT:{i}"], [f"ps{b}"], psb[b][:, (c % 4) * 128:(c % 4) * 128 + n],
                   lhsT=wbig[:, kc, col0 + c * 128:col0 + (c + 1) * 128], rhs=xnT[:, kc, t0:t0 + n],
                   start=(kc == 0), stop=(kc == 7))

    def inproj_tm(i, wbig, col0, ncol, b):
        n, t0 = tile_n(i), tile_t0(i)
        for kc in range(8):
            PE("matmul", ["wbig", f"xnT:{i}"], [f"ps{b}"], psb[b][:n, 0:ncol],
               lhsT=xnT[:, kc, t0:t0 + n], rhs=wbig[:, kc, col0:col0 + ncol], start=(kc == 0), stop=(kc == 7))

    def conv_fm(cb, ckey, dst, dkey, nch, n, wcol0, bcol0, tmpb, tkey):
        for c0 in range(0, nch, 4):
            cs = slice(c0, c0 + 4)

            def wb(col):
                return bc(pc[:, col + c0:col + c0 + 4].unsqueeze(2), [128, 4, n])

            DVE("tensor_tensor", [ckey, "pc"], [dkey], out=dst[:, cs, 0:n], in0=cb[:, cs, 0:n], in1=wb(wcol0), op=ALU.mult)
            for kk in range(1, 4):
                DVE("tensor_tensor", [ckey, "pc"], [tkey], out=tmpb[:, 0:4, 0:n], in0=cb[:, cs, kk:kk + n],
                    in1=wb(wcol0 + kk * nch), op=ALU.mult)
                DVE("tensor_tensor", [dkey, tkey], [dkey], out=dst[:, cs, 0:n], in0=dst[:, cs, 0:n], in1=tmpb[:, 0:4, 0:n],
                    op=ALU.add)
            if bcol0 is not None:
                DVE("tensor_tensor", [dkey, "pc"], [dkey], out=dst[:, cs, 0:n], in0=dst[:, cs, 0:n], in1=wb(bcol0), op=ALU.add)
        ACT("copy", [ckey], [ckey], out=cb[:, :, 0:3], in_=cb[:, :, n:n + 3])

    def gelu_tanh(dst, dkey, src, skey, tmp, tkey, shape_ap):
        ACT("activation", [skey], [tkey], out=tmp, in_=src, func=AF.Square)
        DVE("tensor_scalar", [tkey], [tkey], out=tmp, in0=tmp, scalar1=0.044715, scalar2=1.0, op0=ALU.mult, op1=ALU.add)
        DVE("tensor_tensor", [tkey, skey], [tkey], out=tmp, in0=tmp, in1=src, op=ALU.mult)
        ACT("activation", [tkey], [tkey], out=tmp, in_=tmp, func=AF.Sigmoid, scale=1.5957691216057308)
        DVE("tensor_tensor", [tkey, skey], [dkey], out=dst, in0=tmp, in1=src, op=ALU.mult)

    def chan_rmsnorm_fm(y, ykey, ynT, n, gcol0, sq, rs):
        ACT("activation", [ykey], ["sq"], out=sq[:, :, 0:n], in_=y[:, :, 0:n], func=AF.Square)
        for c in range(4):
            PE("matmul", ["sq", "ones"], ["ps4"], psb[4][:, 0:n], lhsT=ones[:, :], rhs=sq[:, c, 0:n],
               start=(c == 0), stop=(c == 3))
        DVE("tensor_scalar", ["ps4"], ["rs"], out=rs[:, 0:n], in0=psb[4][:, 0:n], scalar1=1.0 / 512, scalar2=EPS,
            op0=ALU.mult, op1=ALU.add)
        ACT("activation", ["rs"], ["rs"], out=rs[:, 0:n], in_=rs[:, 0:n], func=AF.Ln)
        ACT("activation", ["rs"], ["rs"], out=rs[:, 0:n], in_=rs[:, 0:n], func=AF.Exp, scale=-0.5)
        for c in range(4):
            DVE("scalar_tensor_tensor", [ykey, "pc", "rs"], ["ynT"], out=ynT[:, c, 0:n], in0=y[:, c, 0:n],
                scalar=pc[:, gcol0 + c:gcol0 + c + 1], in1=rs[:, 0:n], op0=ALU.mult, op1=ALU.mult)

    def outproj(i, ynT, wo):
        n = tile_n(i)
        for dh in range(2):
            b = 6 + dh
            for c in range(4):
                PE("matmul", ["ynT", "wo"], [f"ps{b}"], psb[b][:n, :], lhsT=ynT[:, c, 0:n],
                   rhs=wo[:, c, dh * 512:(dh + 1) * 512], start=(c == 0), stop=(c == 3))
            DVE("tensor_tensor", [f"ps{b}", f"h:{i}"], [f"h:{i}"], out=h[:n, i, dh * 512:(dh + 1) * 512],
                in0=psb[b][:n, :], in1=h[:n, i, dh * 512:(dh + 1) * 512], op=ALU.add)

    def tm_to_ynT(i, ytm, ykey, ynT):
        n = tile_n(i)
        for c in range(4):
            PE("transpose", [ykey, "ident"], ["ps5"], out=psb[5][:, c * 128:c * 128 + n],
               in_=ytm[:n, c * 128:(c + 1) * 128], identity=ident[:n, :n])
        ACT("copy", ["ps5"], ["ynT"], out=ynT[:, :, 0:n], in_=ps3(5, n))

    def lru_load(l):
        WA.reset()
        wbig = WA.get([128, 8, 1024], BF16)
        wo = WA.get([128, 4, D], BF16)
        wbd = WA.get([128, 2, 4, 128], BF16)
        load_win(wbig, l, 0, 1024)
        load_wout(wo, l, 0)
        POOL("memset", [], ["wbd"], wbd[:, :, :, :], 0.0)
        for wi, nm in enumerate(("lru_w_a", "lru_w_i")):
            for hh in range(8):
                e = hh % 2
                P.dma("pool", wbd[e * 64:(e + 1) * 64, wi, hh // 2, e * 64:(e + 1) * 64], W[nm][l, hh], "D_wbd",
                      reads=[], writes=["wbd"])
        s5w = (WA.get([128, 8, 512], BF16), WA.get([128, 4, D], BF16))
        wv = W["w_in"][l].rearrange("(k p) c -> p k c", p=128)
        P.dma("pool", s5w[0], wv[:, :, 4624:5136], "D_win5", writes=["wbig5"])
        wov = W["w_out"][l].rearrange("(k p) d -> p k d", p=128)
        P.dma("pool", s5w[1], wov[:, 12:16, :], "D_wo5", writes=["wo5"])
        st["lru_w"] = (wbig, wo, wbd)
        st["s5_w"] = s5w

    def lru_phase(l):
        P.barrier()
        MA.reset()
        wbig, wo, wbd = st["lru_w"]
        cb = MA.get([128, 4, 131])
        tmp = MA.get([128, 4, 128])
        gl2 = [MA.get([128, 4, 128]) for _ in range(2)]
        xc2 = [MA.get([128, 4, 128]) for _ in range(2)]
        xcb2 = [MA.get([128, 4, 128], BF16) for _ in range(2)]
        r = MA.get([128, 4, 128])
        ig = MA.get([128, 4, 128])
        aa = MA.get([128, 4, 128])
        mm = MA.get([128, 4, 128])
        hs = MA.get([128, 4, 128])
        sq = MA.get([128, 4, 128])
        rs = MA.get([128, 128])
        ynT = MA.get([128, 4, 128], BF16)
        carry = MA.get([128, 4])
        cl = MA.get([128, 8])
        load_cols([W["lru_conv_w"][l].rearrange("k (c p) -> (k c) p", p=128),
                   W["lru_conv_b"][l].rearrange("(c p) -> c p", p=128),
                   W["lru_b_a"][l].rearrange("(c p) -> c p", p=128),
                   W["lru_b_i"][l].rearrange("(c p) -> c p", p=128),
                   W["lru_lambda"][l].rearrange("(c p) -> c p", p=128),
                   W["lru_norm"][l].rearrange("(c p) -> c p", p=128)])
        ACT("activation", ["pc"], ["cl"], out=cl[:, 0:4], in_=pc[:, 28:32], func=AF.Exp, scale=-1.0)
        ACT("activation", ["cl"], ["cl"], out=cl[:, 0:4], in_=cl[:, 0:4], func=AF.Ln, bias=1.0)
        DVE("tensor_scalar", ["cl"], ["cl"], out=cl[:, 4:8], in0=cl[:, 0:4], scalar1=-16.0, scalar2=None, op0=ALU.mult)
        DVE("tensor_scalar", ["cl"], ["cl"], out=cl[:, 0:4], in0=cl[:, 0:4], scalar1=-8.0, scalar2=None, op0=ALU.mult)
        DVE("memset", [], ["cb"], cb[:, :, 0:3], 0.0)
        DVE("memset", [], ["carry"], carry[:, :], 0.0)
        def lru_front(i):
            n = tile_n(i)
            p = i % 2
            gl, xc, xcb = gl2[p], xc2[p], xcb2[p]
            inproj_fm(i, wbig, 0, 8, [0, 1])
            ACT("copy", ["ps0"], ["cb"], out=cb[:, :, 3:3 + n], in_=ps3(0, n))
            gelu_tanh(gl[:, :, 0:n], f"gl{p}", ps3(1, n), "ps1", tmp[:, :, 0:n], "tmp", None)
            conv_fm(cb, "cb", xc, f"xc{p}", 4, n, 0, 16, tmp, "tmp")
            ACT("copy", [f"xc{p}"], [f"xcb{p}"], out=xcb[:, :, 0:n], in_=xc[:, :, 0:n])

        def lru_back(i):
            n = tile_n(i)
            p = i % 2
            gl, xc, xcb = gl2[p], xc2[p], xcb2[p]
            for wi in range(2):
                for c in range(4):
                    PE("matmul", ["wbd", f"xcb{p}"], [f"ps{2 + wi}"], psb[2 + wi][:, c * 128:c * 128 + n],
                       lhsT=wbd[:, wi, c, :], rhs=xcb[:, c, 0:n], start=True, stop=True)
            for c in range(4):
                ACT("activation", ["ps2", "pc"], ["r"], out=r[:, c, 0:n], in_=psb[2][:, c * 128:c * 128 + n],
                    func=AF.Sigmoid, bias=pc[:, 20 + c:21 + c])
                ACT("activation", ["ps3", "pc"], ["ig"], out=ig[:, c, 0:n], in_=psb[3][:, c * 128:c * 128 + n],
                    func=AF.Sigmoid, bias=pc[:, 24 + c:25 + c])
            for c in range(4):
                ACT("activation", ["r", "cl"], ["aa"], out=aa[:, c, 0:n], in_=r[:, c, 0:n], func=AF.Exp, scale=cl[:, c:c + 1])
                ACT("activation", ["r", "cl"], ["mm"], out=mm[:, c, 0:n], in_=r[:, c, 0:n], func=AF.Exp, scale=cl[:, 4 + c:5 + c])
            DVE("tensor_scalar", ["mm"], ["mm"], out=mm[:, :, 0:n], in0=mm[:, :, 0:n], scalar1=-1.0, scalar2=1.0,
                op0=ALU.mult, op1=ALU.add)
            ACT("activation", ["mm"], ["mm"], out=mm[:, :, 0:n], in_=mm[:, :, 0:n], func=AF.Ln)
            ACT("activation", ["mm"], ["mm"], out=mm[:, :, 0:n], in_=mm[:, :, 0:n], func=AF.Exp, scale=0.5)
            DVE("tensor_tensor", ["ig", f"xc{p}"], ["ig"], out=ig[:, :, 0:n], in0=ig[:, :, 0:n], in1=xc[:, :, 0:n], op=ALU.mult)
            DVE("tensor_tensor", ["ig", "mm"], ["ig"], out=ig[:, :, 0:n], in0=ig[:, :, 0:n], in1=mm[:, :, 0:n], op=ALU.mult)
            for c in range(4):
                DVE("tensor_tensor_scan", ["aa", "ig", "carry"], ["hs"], out=hs[:, c, 0:n], data0=aa[:, c, 0:n],
                    data1=ig[:, c, 0:n], initial=carry[:, c:c + 1], op0=ALU.mult, op1=ALU.add)
            DVE("tensor_copy", ["hs"], ["carry"], out=carry[:, :], in_=hs[:, :, n - 1])
            DVE("tensor_tensor", ["hs", f"gl{p}"], ["hs"], out=hs[:, :, 0:n], in0=hs[:, :, 0:n], in1=gl[:, :, 0:n], op=ALU.mult)
            chan_rmsnorm_fm(hs, "hs", ynT, n, 32, sq, rs)
            outproj(i, ynT, wo)

        lru_front(0)
        for i in range(NT):
            P.capture = []
            lru_back(i)
            A = P.capture
            P.capture = []
            if i + 1 < NT:
                lru_front(i + 1)
            B = P.capture
            P.capture = None
            P.replay_merged(A, B)

    PI = math.pi

    def s5_phase(l):
        P.barrier()
        WA.reset()
        MA.reset()
        wbig, wo = st["s5_w"]
        BT = WA.get([128, 2, 16, 128], BF16)
        CT = WA.get([128, 2, 16, 128], BF16)
        wglu = WA.get([128, 4, 512], BF16)
        P.dma("pool", wglu, W["s5_w_glu"][l].rearrange("(k p) c -> p k c", p=128), "D_wglu", writes=["wglu"])
        load_cols([W["s5_d"][l].rearrange("(c p) -> c p", p=128), W["s5_norm"][l].rearrange("(c p) -> c p", p=128)])
        cosT = MA.get([128, 16, 128])
        sinT = MA.get([128, 16, 128])
        rho = MA.get([128, 16])
        Sre = MA.get([128, 16])
        Sim = MA.get([128, 16])
        mark = MA.off
        sm = {nm: MA.get([128, 16]) for nm in ("are", "aim", "ldt", "th", "c1", "s1", "fre", "fim", "a", "b", "c", "den")}
        bw = [MA.get([128, 16, 128]) for _ in range(2)]
        tt = [MA.get([128, 8, 128]) for _ in range(2)]
        cnat = [MA.get([128, 4, 64]) for _ in range(2)]
        mwork = MA.get([128, 2, 128])
        for nm, key in (("s5_a_re", "are"), ("s5_a_im", "aim")):
            v = W[nm][l].rearrange("(j e) p -> e p j", e=2)
            for e in range(2):
                P.dma("sp", sm[key][e * 64:(e + 1) * 64, :], v[e], "D_s5p", writes=[key])
        ld = W["s5_log_dt"][l]
        for e in range(2):
            src = AP(ld.tensor, ld.offset + e, [[0, 64], [2, 16]])
            P.dma("sp", sm["ldt"][e * 64:(e + 1) * 64, :], src, "D_s5p", writes=["ldt"])
        S = lambda nm: sm[nm][:, :]
        ACT("activation", ["ldt"], ["ldt"], out=S("ldt"), in_=S("ldt"), func=AF.Exp)
        DVE("tensor_scalar", ["are"], ["are"], out=S("are"), in0=S("are"), scalar1=-1e-4, scalar2=None, op0=ALU.min)
        DVE("tensor_tensor", ["ldt", "are"], ["a"], out=S("a"), in0=S("ldt"), in1=S("are"), op=ALU.mult)
        ACT("activation", ["a"], ["rho"], out=rho[:, :], in_=S("a"), func=AF.Exp)
        DVE("tensor_tensor", ["ldt", "aim"], ["th"], out=S("th"), in0=S("ldt"), in1=S("aim"), op=ALU.mult)
        for dst, sh in (("c1", 0.5 * PI), ("s1", 0.0)):
            DVE("tensor_scalar", ["th"], [dst], out=S(dst), in0=S("th"), scalar1=sh, scalar2=None, op0=ALU.add)
            for _ in range(6):
                DVE("tensor_scalar", [dst], ["c"], out=S("c"), in0=S(dst), scalar1=PI, scalar2=-2 * PI, op0=ALU.is_gt, op1=ALU.mult)
                DVE("tensor_tensor", [dst, "c"], [dst], out=S(dst), in0=S(dst), in1=S("c"), op=ALU.add)
            ACT("activation", [dst], [dst], out=S(dst), in_=S(dst), func=AF.Sin)
        DVE("tensor_tensor", ["rho", "c1"], ["a"], out=S("a"), in0=rho[:, :], in1=S("c1"), op=ALU.mult)
        DVE("tensor_scalar", ["a"], ["a"], out=S("a"), in0=S("a"), scalar1=-1.0, scalar2=None, op0=ALU.add)
        DVE("tensor_tensor", ["rho", "s1"], ["b"], out=S("b"), in0=rho[:, :], in1=S("s1"), op=ALU.mult)
        DVE("tensor_tensor", ["are"], ["den"], out=S("den"), in0=S("are"), in1=S("are"), op=ALU.mult)
        DVE("tensor_tensor", ["aim"], ["c"], out=S("c"), in0=S("aim"), in1=S("aim"), op=ALU.mult)
        DVE("tensor_tensor", ["den", "c"], ["den"], out=S("den"), in0=S("den"), in1=S("c"), op=ALU.add)
        DVE("reciprocal", ["den"], ["den"], out=S("den"), in_=S("den"))
        DVE("tensor_tensor", ["a", "are"], ["fre"], out=S("fre"), in0=S("a"), in1=S("are"), op=ALU.mult)
        DVE("tensor_tensor", ["b", "aim"], ["c"], out=S("c"), in0=S("b"), in1=S("aim"), op=ALU.mult)
        DVE("tensor_tensor", ["fre", "c"], ["fre"], out=S("fre"), in0=S("fre"), in1=S("c"), op=ALU.add)
        DVE("tensor_tensor", ["fre", "den"], ["fre"], out=S("fre"), in0=S("fre"), in1=S("den"), op=ALU.mult)
        DVE("tensor_tensor", ["b", "are"], ["fim"], out=S("fim"), in0=S("b"), in1=S("are"), op=ALU.mult)
        DVE("tensor_tensor", ["a", "aim"], ["c"], out=S("c"), in0=S("a"), in1=S("aim"), op=ALU.mult)
        DVE("tensor_tensor", ["fim", "c"], ["fim"], out=S("fim"), in0=S("fim"), in1=S("c"), op=ALU.subtract)
        DVE("tensor_tensor", ["fim", "den"], ["fim"], out=S("fim"), in0=S("fim"), in1=S("den"), op=ALU.mult)
        DVE("tensor_copy", ["c1"], ["cosT"], out=cosT[:, :, 0], in_=S("c1"))
        DVE("tensor_copy", ["s1"], ["sinT"], out=sinT[:, :, 0], in_=S("s1"))
        m = 1
        t1 = tt[0][:, :, :].rearrange("p a b -> p (a b)").rearrange("p (a b) -> p a b", a=16)
        t2 = tt[1][:, :, :].rearrange("p a b -> p (a b)").rearrange("p (a b) -> p a b", a=16)
        while m < 128:
            cm = bc(cosT[:, :, m - 1:m], [128, 16, m])
            smm = bc(sinT[:, :, m - 1:m], [128, 16, m])
            DVE("tensor_tensor", ["cosT"], ["t1"], out=t1[:, :, 0:m], in0=cosT[:, :, 0:m], in1=cm, op=ALU.mult)
            DVE("tensor_tensor", ["sinT"], ["t2"], out=t2[:, :, 0:m], in0=sinT[:, :, 0:m], in1=smm, op=ALU.mult)
            DVE("tensor_tensor", ["t1", "t2"], ["cosN"], out=cosT[:, :, m:2 * m], in0=t1[:, :, 0:m], in1=t2[:, :, 0:m], op=ALU.subtract)
            DVE("tensor_tensor", ["sinT"], ["t1"], out=t1[:, :, 0:m], in0=sinT[:, :, 0:m], in1=cm, op=ALU.mult)
            DVE("tensor_tensor", ["cosT"], ["t2"], out=t2[:, :, 0:m], in0=cosT[:, :, 0:m], in1=smm, op=ALU.mult)
            DVE("tensor_tensor", ["t1", "t2"], ["sinT"], out=sinT[:, :, m:2 * m], in0=t1[:, :, 0:m], in1=t2[:, :, 0:m], op=ALU.add)
            DVE("engine_nop", ["cosN"], ["cosT"])
            m *= 2
        for part, nm in enumerate(("s5_b_re", "s5_b_im")):
            POOL("memset", [], [f"bw{part}"], bw[part][:, :, :], 0.0)
            src0 = W[nm][l]
            for e in range(2):
                dv = bw[part][e * 64:(e + 1) * 64, :, :]
                for a4 in range(4):
                    dst = AP(dv.tensor, dv.offset + e * 16 + a4 * 512, [list(dv.ap[0]), [160, 4], [1, 16]])
                    src = AP(src0.tensor, src0.offset + e * 1024 + a4 * 8192, [[16, 64], [2048, 4], [1, 16]])
                    P.dma("sp", dst, src, "D_s5b", writes=[f"bw{part}"])
        for hf in range(2):
            js = slice(hf * 8, hf * 8 + 8)
            fre_b = bc(sm["fre"][:, js].unsqueeze(2), [128, 8, 128])
            fim_b = bc(sm["fim"][:, js].unsqueeze(2), [128, 8, 128])
            DVE("tensor_tensor", ["bw0", "fre"], ["t1"], out=tt[0][:, :, :], in0=bw[0][:, js, :], in1=fre_b, op=ALU.mult)
            DVE("tensor_tensor", ["bw1", "fim"], ["t2"], out=tt[1][:, :, :], in0=bw[1][:, js, :], in1=fim_b, op=ALU.mult)
            DVE("tensor_tensor", ["t1", "t2"], ["t1"], out=tt[0][:, :, :], in0=tt[0][:, :, :], in1=tt[1][:, :, :], op=ALU.subtract)
            DVE("tensor_tensor", ["bw0", "fim"], ["t2"], out=tt[1][:, :, :], in0=bw[0][:, js, :], in1=fim_b, op=ALU.mult)
            DVE("tensor_copy", ["t1"], ["bw0"], out=bw[0][:, js, :], in_=tt[0][:, :, :])
            DVE("tensor_tensor", ["bw1", "fre"], ["t1"], out=tt[0][:, :, :], in0=bw[1][:, js, :], in1=fre_b, op=ALU.mult)
            DVE("tensor_tensor", ["t1", "t2"], ["bw1"], out=bw[1][:, js, :], in0=tt[0][:, :, :], in1=tt[1][:, :, :], op=ALU.add)
        for part in range(2):
            for j4 in range(4):
                b = (part * 4 + j4) % 4
                for q in range(4):
                    j = j4 * 4 + q
                    PE("transpose", [f"bw{part}", "ident"], [f"ps{b}"], out=psb[b][:, q * 128:(q + 1) * 128],
                       in_=bw[part][:, j, :], identity=ident[:, :])
                ACT("copy", [f"ps{b}"], ["BT"], out=BT[:, part, j4 * 4:j4 * 4 + 4, :], in_=ps3(b, 128))
        for part, nm in enumerate(("s5_c_re", "s5_c_im")):
            P.dma("sp", cnat[part], W[nm][l].rearrange("(G g) i p -> (g i) G p", G=4), "D_s5c", writes=[f"cnat{part}"])
        it = 0
        for part in range(2):
            for j4 in range(4):
                b = 4 + (part * 4 + j4) % 4
                for q in range(4):
                    mw = it % 2
                    it += 1
                    for e in range(2):
                        DVE("tensor_scalar", [f"cnat{part}", "gmask"], [f"mw{mw}"], out=mwork[:, mw, e * 64:(e + 1) * 64],
                            in0=cnat[part][:, j4, :], scalar1=gmask[:, 2 * q + e:2 * q + e + 1], scalar2=None, op0=ALU.mult)
                    PE("transpose", [f"mw{mw}", "ident"], [f"ps{b}"], out=psb[b][:, q * 128:(q + 1) * 128],
                       in_=mwork[:, mw, :], identity=ident[:, :])
                ACT("mul", [f"ps{b}"], ["CT"], out=CT[:, part, j4 * 4:j4 * 4 + 4, :], in_=ps3(b, 128),
                    mul=(1.0 if part == 0 else -1.0))
        P.barrier()
        MA.off = mark
        uT = MA.get([128, 4, 128])
        uTb = MA.get([128, 4, 128], BF16)
        t1 = MA.get([128, 4, 128])
        t2 = MA.get([128, 4, 128])
        zre = MA.get([128, 4, 128])
        zim = MA.get([128, 4, 128])
        wre = MA.get([128, 4, 128])
        wim = MA.get([128, 4, 128])
        sreb = MA.get([128, 4, 128], BF16)
        simb = MA.get([128, 4, 128], BF16)
        yv = MA.get([128, 4, 128])
        yg = MA.get([128, 4, 128])
        tmp = MA.get([128, 4, 128])
        sq = MA.get([128, 4, 128])
        ygb = MA.get([128, 4, 128], BF16)
        ynT = MA.get([128, 4, 128], BF16)
        rs = MA.get([128, 128])
        s1 = MA.get([128, 4])
        s2 = MA.get([128, 4])
        DVE("memset", [], ["Sre"], Sre[:, :], 0.0)
        DVE("memset", [], ["Sim"], Sim[:, :], 0.0)

        def s5_front(i):
            n = tile_n(i)
            inproj_fm(i, wbig, 0, 4, [0])
            ACT("copy", ["ps0"], ["uT"], out=uT[:, :, 0:n], in_=ps3(0, n))
            ACT("copy", ["ps0"], ["uTb"], out=uTb[:, :, 0:n], in_=ps3(0, n))

        def s5_G(i, G):
            n = tile_n(i)
            bR = 1 + (G % 2) * 2
            bI = bR + 1
            for q in range(4):
                j = 4 * G + q
                PE("matmul", ["BT", "uTb"], [f"ps{bR}"], psb[bR][:, q * 128:q * 128 + n], lhsT=BT[:, 0, j, :],
                   rhs=uTb[:, G, 0:n], start=True, stop=True)
                PE("matmul", ["BT", "uTb"], [f"ps{bI}"], psb[bI][:, q * 128:q * 128 + n], lhsT=BT[:, 1, j, :],
                   rhs=uTb[:, G, 0:n], start=True, stop=True)
            cG = cosT[:, 4 * G:4 * G + 4, 0:n]
            sG = sinT[:, 4 * G:4 * G + 4, 0:n]
            pR, pI = ps3(bR, n), ps3(bI, n)
            T1, T2 = t1[:, :, 0:n], t2[:, :, 0:n]
            DVE("tensor_tensor", [f"ps{bR}"], ["t1"], out=T1, in0=cG, in1=pR, op=ALU.mult)
            DVE("tensor_tensor", [f"ps{bI}"], ["t2"], out=T2, in0=sG, in1=pI, op=ALU.mult)
            DVE("tensor_tensor", ["t1", "t2"], ["zre"], out=zre[:, :, 0:n], in0=T1, in1=T2, op=ALU.add)
            DVE("tensor_tensor", [f"ps{bI}"], ["t1"], out=T1, in0=cG, in1=pI, op=ALU.mult)
            DVE("tensor_tensor", [f"ps{bR}"], ["t2"], out=T2, in0=sG, in1=pR, op=ALU.mult)
            DVE("tensor_tensor", ["t1", "t2"], ["zim"], out=zim[:, :, 0:n], in0=T1, in1=T2, op=ALU.subtract)
            for q in range(4):
                j = 4 * G + q
                rb = bc(rho[:, j:j + 1], [128, n])
                DVE("tensor_tensor_scan", ["zre", "Sre"], ["wre"], out=wre[:, q, 0:n], data0=rb, data1=zre[:, q, 0:n],
                    initial=Sre[:, j:j + 1], op0=ALU.mult, op1=ALU.add)
                DVE("tensor_tensor_scan", ["zim", "Sim"], ["wim"], out=wim[:, q, 0:n], data0=rb, data1=zim[:, q, 0:n],
                    initial=Sim[:, j:j + 1], op0=ALU.mult, op1=ALU.add)
            cl_, sl_ = cosT[:, 4 * G:4 * G + 4, n - 1], sinT[:, 4 * G:4 * G + 4, n - 1]
            wrl, wil = wre[:, :, n - 1], wim[:, :, n - 1]
            DVE("tensor_tensor", ["wre"], ["s1"], out=s1[:, :], in0=cl_, in1=wrl, op=ALU.mult)
            DVE("tensor_tensor", ["wim"], ["s2"], out=s2[:, :], in0=sl_, in1=wil, op=ALU.mult)
            DVE("tensor_tensor", ["s1", "s2"], ["Sre"], out=Sre[:, 4 * G:4 * G + 4], in0=s1[:, :], in1=s2[:, :], op=ALU.subtract)
            DVE("tensor_tensor", ["wre"], ["s1"], out=s1[:, :], in0=sl_, in1=wrl, op=ALU.mult)
            DVE("tensor_tensor", ["wim"], ["s2"], out=s2[:, :], in0=cl_, in1=wil, op=ALU.mult)
            DVE("tensor_tensor", ["s1", "s2"], ["Sim"], out=Sim[:, 4 * G:4 * G + 4], in0=s1[:, :], in1=s2[:, :], op=ALU.add)
            WR, WI = wre[:, :, 0:n], wim[:, :, 0:n]
            DVE("tensor_tensor", ["wre"], ["t1"], out=T1, in0=cG, in1=WR, op=ALU.mult)
            DVE("tensor_tensor", ["wim"], ["t2"], out=T2, in0=sG, in1=WI, op=ALU.mult)
            DVE("tensor_tensor", ["t1", "t2"], ["sreb"], out=sreb[:, :, 0:n], in0=T1, in1=T2, op=ALU.subtract)
            DVE("tensor_tensor", ["wre"], ["t1"], out=T1, in0=sG, in1=WR, op=ALU.mult)
            DVE("tensor_tensor", ["wim"], ["t2"], out=T2, in0=cG, in1=WI, op=ALU.mult)
            DVE("tensor_tensor", ["t1", "t2"], ["simb"], out=simb[:, :, 0:n], in0=T1, in1=T2, op=ALU.add)
            for q in range(4):
                j = 4 * G + q
                PE("matmul", ["CT", "sreb"], ["ps5"], psb[5][:, G * 128:G * 128 + n], lhsT=CT[:, 0, j, :],
                   rhs=sreb[:, q, 0:n], start=(q == 0), stop=False)
                PE("matmul", ["CT", "simb"], ["ps5"], psb[5][:, G * 128:G * 128 + n], lhsT=CT[:, 1, j, :],
                   rhs=simb[:, q, 0:n], start=False, stop=(q == 3))

        def s5_tail(i, seg):
            n = tile_n(i)
            YV, YG, TMP = yv[:, :, 0:n], yg[:, :, 0:n], tmp[:, :, 0:n]
            if seg == 0:
                for c in range(4):
                    DVE("scalar_tensor_tensor", ["uT", "pc", "ps5"], ["yv"], out=yv[:, c, 0:n], in0=uT[:, c, 0:n],
                        scalar=pc[:, c:c + 1], in1=psb[5][:, c * 128:c * 128 + n], op0=ALU.mult, op1=ALU.add)
                ACT("activation", ["yv"], ["tmp"], out=TMP, in_=YV, func=AF.Square)
            elif seg == 1:
                DVE("tensor_scalar", ["tmp"], ["tmp"], out=TMP, in0=TMP, scalar1=0.044715, scalar2=1.0, op0=ALU.mult, op1=ALU.add)
                DVE("tensor_tensor", ["tmp", "yv"], ["tmp"], out=TMP, in0=TMP, in1=YV, op=ALU.mult)
                ACT("activation", ["tmp"], ["tmp"], out=TMP, in_=TMP, func=AF.Sigmoid, scale=1.5957691216057308)
            elif seg == 2:
                DVE("tensor_tensor", ["tmp", "yv"], ["yg"], out=YG, in0=TMP, in1=YV, op=ALU.mult)
                ACT("copy", ["yg"], ["ygb"], out=ygb[:, :, 0:n], in_=YG)
                for oc in range(4):
                    for kc in range(4):
                        PE("matmul", ["wglu", "ygb"], ["ps6"], psb[6][:, oc * 128:oc * 128 + n],
                           lhsT=wglu[:, kc, oc * 128:(oc + 1) * 128], rhs=ygb[:, kc, 0:n], start=(kc == 0), stop=(kc == 3))
                ACT("activation", ["ps6"], ["tmp"], out=TMP, in_=ps3(6, n), func=AF.Sigmoid)
            elif seg == 3:
                DVE("tensor_tensor", ["yg", "tmp"], ["yg"], out=YG, in0=YG, in1=TMP, op=ALU.mult)
                ACT("activation", ["yg"], ["sq"], out=sq[:, :, 0:n], in_=YG, func=AF.Square)
                for c in range(4):
                    PE("matmul", ["sq", "ones"], ["ps7"], psb[7][:, 0:n], lhsT=ones[:, :], rhs=sq[:, c, 0:n],
                       start=(c == 0), stop=(c == 3))
            elif seg == 4:
                DVE("tensor_scalar", ["ps7"], ["rs"], out=rs[:, 0:n], in0=psb[7][:, 0:n], scalar1=1.0 / 512, scalar2=EPS,
                    op0=ALU.mult, op1=ALU.add)
                ACT("activation", ["rs"], ["rs"], out=rs[:, 0:n], in_=rs[:, 0:n], func=AF.Ln)
                ACT("activation", ["rs"], ["rs"], out=rs[:, 0:n], in_=rs[:, 0:n], func=AF.Exp, scale=-0.5)
            elif seg == 5:
                for c in range(4):
                    DVE("scalar_tensor_tensor", ["yg", "pc", "rs"], ["ynT"], out=ynT[:, c, 0:n], in0=yg[:, c, 0:n],
                        scalar=pc[:, 4 + c:5 + c], in1=rs[:, 0:n], op0=ALU.mult, op1=ALU.mult)
                outproj(i, ynT, wo)

        for i in range(NT + 1):
            if i > 0:
                s5_tail(i - 1, 0)
            if i < NT:
                s5_front(i)
            for G in range(4):
                if i < NT:
                    s5_G(i, G)
                if i > 0:
                    s5_tail(i - 1, G + 1)
            if i > 0:
                s5_tail(i - 1, 5)

    def v3(ap, a):
        return ap.rearrange("p (a b) -> p a b", a=a)

    def load_rb(rb, items, l):
        for off, nm, ln in items:
            v = W[nm][l]
            src = AP(v.tensor, v.offset, [[0, 128], [1, ln]])
            P.dma("sp", rb[:, off:off + ln], src, "D_rb", writes=["rb"])

    def ssd_phase(l):
        P.barrier()
        WA.reset()
        MA.reset()
        wbig = WA.get([128, 8, 1544], BF16)
        wo = WA.get([128, 4, D], BF16)
        load_win(wbig, l, 3080, 4624)
        load_wout(wo, l, 1024)
        load_cols([W["ssd_conv_w"][l].rearrange("k (c p) -> (k c) p", p=128),
                   W["ssd_conv_b"][l].rearrange("(c p) -> c p", p=128)])
        rb = MA.get([128, 536])
        load_rb(rb, ((0, "ssd_a_log", 8), (8, "ssd_dt_bias", 8), (16, "ssd_d", 8), (24, "ssd_norm", 512)), l)
        ACT("activation", ["rb"], ["rb"], out=rb[:, 0:8], in_=rb[:, 0:8], func=AF.Exp)
        DVE("tensor_scalar", ["rb"], ["rb"], out=rb[:, 0:8], in0=rb[:, 0:8], scalar1=-1.0, scalar2=None, op0=ALU.mult)
        cb = MA.get([128, 8, 131])
        xbc = MA.get([128, 8, 128])
        ctmp = MA.get([128, 4, 128])
        bcT2 = [MA.get([128, 4, 128], BF16) for _ in range(2)]
        xs_tm2 = [MA.get([128, 512]) for _ in range(2)]
        B_tm2 = [MA.get([128, 256], BF16) for _ in range(2)]
        zs2 = [MA.get([128, 512]) for _ in range(2)]
        smf = [MA.get([128, 16]) for _ in range(2)]
        sm = MA.get([128, 48])
        acum, eac, dst_, dtd, cd, ss2 = (sm[:, 8 * q:8 * q + 8] for q in range(6))
        adtb = MA.get([128, 8, 128])
        DT = MA.get([128, 8, 128])
        ST = MA.get([128, 8, 128], BF16)
        xdt = MA.get([128, 512], BF16)
        xdtd = MA.get([128, 512], BF16)
        Sst = MA.get([128, 512])
        Sb = MA.get([128, 2, 256], BF16)
        y = MA.get([128, 512])
        y1 = MA.get([128, 512])
        ynT = MA.get([128, 4, 128], BF16)
        DVE("memset", [], ["cbS"], cb[:, :, 0:3], 0.0)
        DVE("memset", [], ["Sst"], Sst[:, :], 0.0)
        DVE("memset", [], ["Sb"], Sb[:, :, :], 0.0)

        def ssd_front(i):
            n = tile_n(i)
            p = i % 2
            dtv, adt = smf[p][:, 0:8], smf[p][:, 8:16]
            bcT, xs_tm, B_tm, zs = bcT2[p], xs_tm2[p], B_tm2[p], zs2[p]
            inproj_tm(i, wbig, 0, 512, 0)
            inproj_tm(i, wbig, 1536, 8, 1)
            DVE("tensor_tensor", ["ps1", "rb"], [f"dtv{p}"], out=dtv[:n, :], in0=psb[1][:n, 0:8], in1=rb[:n, 8:16], op=ALU.add)
            ACT("activation", [f"dtv{p}"], [f"dtv{p}"], out=dtv[:n, :], in_=dtv[:n, :], func=AF.Exp)
            ACT("activation", [f"dtv{p}"], [f"dtv{p}"], out=dtv[:n, :], in_=dtv[:n, :], func=AF.Ln, bias=1.0)
            DVE("tensor_tensor", [f"dtv{p}", "rb"], [f"adt{p}"], out=adt[:n, :], in0=dtv[:n, :], in1=rb[:n, 0:8], op=ALU.mult)
            inproj_fm(i, wbig, 512, 8, [2, 3])
            ACT("copy", ["ps2"], ["cbS"], out=cb[:, 0:4, 3:3 + n], in_=ps3(2, n))
            ACT("copy", ["ps3"], ["cbS"], out=cb[:, 4:8, 3:3 + n], in_=ps3(3, n))
            conv_fm(cb, "cbS", xbc, "xbc", 8, n, 0, 32, ctmp, "ctmp")
            ACT("activation", ["xbc"], ["xbc"], out=xbc[:, :, 0:n], in_=xbc[:, :, 0:n], func=AF.Silu)
            ACT("activation", ["ps0"], [f"zs{p}"], out=zs[:n, :], in_=psb[0][:n, 0:512], func=AF.Silu)
            ACT("copy", ["xbc"], [f"bcT{p}"], out=bcT[:, :, 0:n], in_=xbc[:, 4:8, 0:n])
            for c in range(4):
                PE("transpose", ["xbc", "ident"], ["ps2"], out=psb[2][:n, c * 128:(c + 1) * 128], in_=xbc[:, c, 0:n],
                   identity=ident[:, :])
            for c in range(2):
                PE("transpose", ["xbc", "ident"], ["ps3"], out=psb[3][:n, c * 128:(c + 1) * 128], in_=xbc[:, 4 + c, 0:n],
                   identity=ident[:, :])
            DVE("tensor_copy", ["ps2"], [f"xs_tm{p}"], out=xs_tm[:n, :], in_=psb[2][:n, 0:512])
            ACT("copy", ["ps3"], [f"B_tm{p}"], out=B_tm[:n, :], in_=psb[3][:n, 0:256])

        def ssd_back(i):
            n = tile_n(i)
            p = i % 2
            dtv, adt = smf[p][:, 0:8], smf[p][:, 8:16]
            bcT, xs_tm, B_tm, zs = bcT2[p], xs_tm2[p], B_tm2[p], zs2[p]
            kb, kx, kB, kz, kd, ka = f"bcT{p}", f"xs_tm{p}", f"B_tm{p}", f"zs{p}", f"dtv{p}", f"adt{p}"
            PE("matmul", [ka, "triu"], ["ps5"], psb[5][:n, 0:8], lhsT=triu[:n, :n], rhs=adt[:n, :], start=True, stop=True)
            DVE("tensor_copy", ["ps5"], ["acum"], out=acum[:n, :], in_=psb[5][:n, 0:8])
            ACT("activation", ["ps5"], ["eac"], out=eac[:n, :], in_=psb[5][:n, 0:8], func=AF.Exp)
            DVE("tensor_copy", [ka], ["adtb"], out=adtb[:n, :, :], in_=bc(adt[:n, :].unsqueeze(2), [n, 8, 128]))
            for hh in range(8):
                b = 6 + hh // 4
                PE("matmul", ["adtb", "triu"], [f"ps{b}"], psb[b][:, (hh % 4) * 128:(hh % 4) * 128 + n],
                   lhsT=adtb[:n, hh, :], rhs=triu[:n, :n], start=True, stop=True)
            for hh in range(8):
                b = 6 + hh // 4
                DVE("scalar_tensor_tensor", [f"ps{b}", "acum", "mincT"], ["DT"], out=DT[:n, hh, 0:n],
                    in0=psb[b][:n, (hh % 4) * 128:(hh % 4) * 128 + n], scalar=acum[:n, hh:hh + 1], in1=mincT[:n, :n],
                    op0=ALU.subtract, op1=ALU.add)
            ACT("activation", ["DT"], ["DT"], out=DT[:n, :, 0:n], in_=DT[:n, :, 0:n], func=AF.Exp)
            for hb in range(2):
                pv = v3(psb[6 + hb][:, :], 4)
                ACT("activation", [f"ps{6 + hb}"], ["cd"], out=cd[:, 4 * hb:4 * hb + 4], in_=pv[:, :, n - 1], func=AF.Exp)
                DVE("tensor_tensor", [f"ps{6 + hb}", "acum"], ["dst"], out=dst_[:n, 4 * hb:4 * hb + 4], in0=pv[:n, :, n - 1],
                    in1=acum[:n, 4 * hb:4 * hb + 4], op=ALU.subtract)
            ACT("activation", ["dst"], ["dst"], out=dst_[:n, :], in_=dst_[:n, :], func=AF.Exp)
            DVE("tensor_tensor", [kd, "dst"], ["dtd"], out=dtd[:n, :], in0=dtv[:n, :], in1=dst_[:n, :], op=ALU.mult)
            for g in range(2):
                PE("matmul", [kb], ["ps4"], psb[4][:n, g * 128:g * 128 + n], lhsT=bcT[:, g, 0:n], rhs=bcT[:, 2 + g, 0:n],
                   start=True, stop=True)
            for g in range(2):
                DVE("tensor_tensor", ["DT", "ps4"], ["ST"], out=ST[:n, 4 * g:4 * g + 4, 0:n], in0=DT[:n, 4 * g:4 * g + 4, 0:n],
                    in1=bc(psb[4][:n, g * 128:g * 128 + n].unsqueeze(1), [n, 4, n]), op=ALU.mult)
            DVE("tensor_tensor", [kx, kd], ["xdt"], out=v3(xdt[:n, :], 8), in0=v3(xs_tm[:n, :], 8),
                in1=bc(dtv[:n, :].unsqueeze(2), [n, 8, 64]), op=ALU.mult)
            DVE("tensor_tensor", [kx, "dtd"], ["xdtd"], out=v3(xdtd[:n, :], 8), in0=v3(xs_tm[:n, :], 8),
                in1=bc(dtd[:n, :].unsqueeze(2), [n, 8, 64]), op=ALU.mult)
            for hh in range(8):
                PE("matmul", ["ST", "xdt"], ["ps5"], psb[5][:n, hh * 64:(hh + 1) * 64], lhsT=ST[:n, hh, 0:n],
                   rhs=xdt[:n, hh * 64:(hh + 1) * 64], start=True, stop=True)
            for g in range(2):
                PE("matmul", [kb, "Sb"], ["ps4"], psb[4][:n, g * 256:(g + 1) * 256], lhsT=bcT[:, 2 + g, 0:n],
                   rhs=Sb[:, g, :], start=True, stop=True)
            for g in range(2):
                PE("matmul", [kB, "xdtd"], ["ps6"], psb[6][:, g * 256:(g + 1) * 256], lhsT=B_tm[:n, g * 128:(g + 1) * 128],
                   rhs=xdtd[:n, g * 256:(g + 1) * 256], start=True, stop=True)
            DVE("tensor_tensor", ["Sst", "cd"], ["Sst"], out=v3(Sst[:, :], 8), in0=v3(Sst[:, :], 8),
                in1=bc(cd[:, :].unsqueeze(2), [128, 8, 64]), op=ALU.mult)
            DVE("tensor_tensor", ["Sst", "ps6"], ["Sst"], out=Sst[:, :], in0=Sst[:, :], in1=psb[6][:, 0:512], op=ALU.add)
            ACT("copy", ["Sst"], ["Sb"], out=Sb[:, :, :], in_=v3(Sst[:, :], 2))
            DVE("tensor_tensor", ["ps4", "eac"], ["y1"], out=v3(y1[:n, :], 8), in0=v3(psb[4][:n, 0:512], 8),
                in1=bc(eac[:n, :].unsqueeze(2), [n, 8, 64]), op=ALU.mult)
            DVE("tensor_tensor", ["y1", "ps5"], ["y"], out=y[:n, :], in0=y1[:n, :], in1=psb[5][:n, 0:512], op=ALU.add)
            DVE("tensor_tensor", [kx, "rb"], ["y1"], out=v3(y1[:n, :], 8), in0=v3(xs_tm[:n, :], 8),
                in1=bc(rb[:n, 16:24].unsqueeze(2), [n, 8, 64]), op=ALU.mult)
            DVE("tensor_tensor", ["y", "y1"], ["y"], out=y[:n, :], in0=y[:n, :], in1=y1[:n, :], op=ALU.add)
            DVE("tensor_tensor", ["y", kz], ["y"], out=y[:n, :], in0=y[:n, :], in1=zs[:n, :], op=ALU.mult)
            for g in range(2):
                ACT("activation", ["y"], ["junk", "ss2"], out=junk[:n, 0:256], in_=y[:n, g * 256:(g + 1) * 256], func=AF.Square,
                    accum_out=ss2[:n, g:g + 1])
            rsqrt_inplace(ss2[:n, 0:2], "ss2", 1.0 / 256, EPS)
            for g in range(2):
                DVE("scalar_tensor_tensor", ["y", "ss2", "rb"], ["y"], out=y[:n, g * 256:(g + 1) * 256],
                    in0=y[:n, g * 256:(g + 1) * 256], scalar=ss2[:n, g:g + 1], in1=rb[:n, 24 + g * 256:24 + (g + 1) * 256],
                    op0=ALU.mult, op1=ALU.mult)
            tm_to_ynT(i, y, "y", ynT)
            outproj(i, ynT, wo)

        ssd_front(0)
        for i in range(NT):
            P.capture = []
            ssd_back(i)
            A = P.capture
            P.capture = []
            if i + 1 < NT:
                ssd_front(i + 1)
            B = P.capture
            P.capture = None
            P.replay_merged(A, B)

    def gdn_phase(l):
        P.barrier()
        WA.reset()
        MA.reset()
        wbig = WA.get([128, 8, 2056], BF16)
        wo = WA.get([128, 4, D], BF16)
        load_win(wbig, l, 1024, 3080)
        load_wout(wo, l, 512)
        load_cols([W["gdn_conv_w"][l].rearrange("k (c p) -> (k c) p", p=128)])
        rb = MA.get([128, 136])
        load_rb(rb, ((0, "gdn_a_log", 4), (4, "gdn_dt_bias", 4), (8, "gdn_norm", 128)), l)
        ACT("activation", ["rb"], ["rb"], out=rb[:, 0:4], in_=rb[:, 0:4], func=AF.Exp)
        DVE("tensor_scalar", ["rb"], ["rb"], out=rb[:, 0:4], in0=rb[:, 0:4], scalar1=-1.0, scalar2=None, op0=ALU.mult)
        cb = MA.get([128, 12, 131])
        qkv = MA.get([128, 12, 128])
        sq = MA.get([128, 8, 128])
        qT = MA.get([128, 4, 128], BF16)
        kT = MA.get([128, 4, 128], BF16)
        qdT = MA.get([128, 4, 128], BF16)
        zs = MA.get([128, 512])
        sm = MA.get([128, 40])
        bg, gcum, egl, kds, egc, bge, ss4 = (sm[:, 0:8], sm[:, 8:12], sm[:, 12:16], sm[:, 16:20], sm[:, 20:24],
                                             sm[:, 24:28], sm[:, 28:32])
        gb = MA.get([128, 4, 128])
        DTi = MA.get([128, 4, 128])
        Dst = MA.get([128, 4, 128])
        erow = MA.get([128, 4, 128])
        vb = MA.get([128, 4, 128])
        kbg = MA.get([128, 4, 128])
        kdec = MA.get([128, 4, 128], BF16)
        XS = Arena(xs[:, :, :].rearrange("p a b -> p (a b)"), 2048)
        Xb = [XS.get([128, 128]) for _ in range(4)]
        XTb = [XS.get([128, 128]) for _ in range(4)]
        PTh = [XS.get([128, 128]) for _ in range(4)]
        u_sh = [XS.get([128, 128]) for _ in range(4)]
        attnTh = [MA.get([128, 128], BF16) for _ in range(4)]
        wTh = [MA.get([128, 128], BF16) for _ in range(4)]
        vnewh = [MA.get([128, 128], BF16) for _ in range(4)]
        Sf = MA.get([128, 4, 128])
        Sb = MA.get([128, 4, 128], BF16)
        o = MA.get([128, 512])
        ynT = MA.get([128, 4, 128], BF16)
        DVE("memset", [], ["cbG"], cb[:, :, 0:3], 0.0)
        DVE("memset", [], ["Sf"], Sf[:, :, :], 0.0)
        DVE("memset", [], ["Sb"], Sb[:, :, :], 0.0)
        def gdn_Y(i):
            n = tile_n(i)
            nlev = int(round(math.log2(n))) - 1
            F32R = None
            for rb_ in range(3):
                inproj_fm(i, wbig, rb_ * 512, 4, [6])
                ACT("copy", ["ps6"], ["cbG"], out=cb[:, 4 * rb_:4 * rb_ + 4, 3:3 + n], in_=ps3(6, n))
            conv_fm(cb, "cbG", qkv, "qkv", 12, n, 0, None, sq, "sq")
            ACT("activation", ["qkv"], ["qkv"], out=qkv[:, :, 0:n], in_=qkv[:, :, 0:n], func=AF.Silu)

        def gdn_pre(i):
            n = tile_n(i)
            nlev = int(round(math.log2(n))) - 1
            F32R = None
            inproj_tm(i, wbig, 1536, 512, 7)
            ACT("activation", ["ps7"], ["zs"], out=zs[:n, :], in_=psb[7][:n, 0:512], func=AF.Silu)
            ACT("activation", ["qkv"], ["sq"], out=sq[:, :, 0:n], in_=qkv[:, 0:8, 0:n], func=AF.Square)
            for c in range(8):
                b = 3 + c // 4
                PE("matmul", ["sq", "ones"], [f"ps{b}"], psb[b][:, (c % 4) * 128:(c % 4) * 128 + n], lhsT=ones[:, :],
                   rhs=sq[:, c, 0:n], start=True, stop=True)
            for b in range(2):
                DVE("tensor_scalar", [f"ps{3 + b}"], ["sq"], out=sq[:, 4 * b:4 * b + 4, 0:n], in0=ps3(3 + b, n), scalar1=EPS,
                    scalar2=None, op0=ALU.add)
            ACT("activation", ["sq"], ["sq"], out=sq[:, :, 0:n], in_=sq[:, :, 0:n], func=AF.Ln)
            ACT("activation", ["sq"], ["sq"], out=sq[:, :, 0:n], in_=sq[:, :, 0:n], func=AF.Exp, scale=-0.5)
            DVE("tensor_tensor", ["qkv", "sq"], ["qkv"], out=qkv[:, 0:8, 0:n], in0=qkv[:, 0:8, 0:n], in1=sq[:, :, 0:n], op=ALU.mult)
            ACT("mul", ["qkv"], ["qT"], out=qT[:, :, 0:n], in_=qkv[:, 0:4, 0:n], mul=128.0 ** -0.5)
            ACT("copy", ["qkv"], ["kT"], out=kT[:, :, 0:n], in_=qkv[:, 4:8, 0:n])
            for c in range(4):
                PE("transpose", ["qkv", "ident"], ["ps5"], out=psb[5][:n, c * 128:(c + 1) * 128], in_=qkv[:, 4 + c, 0:n],
                   identity=ident[:, :])
            for c in range(4):
                PE("transpose", ["qkv", "ident"], ["ps6"], out=psb[6][:n, c * 128:(c + 1) * 128], in_=qkv[:, 8 + c, 0:n],
                   identity=ident[:, :])
            inproj_tm(i, wbig, 2048, 8, 0)
            ACT("activation", ["ps0"], ["bg"], out=bg[:n, 0:4], in_=psb[0][:n, 0:4], func=AF.Exp, scale=-1.0)
            DVE("tensor_scalar", ["bg"], ["bg"], out=bg[:n, 0:4], in0=bg[:n, 0:4], scalar1=1.0, scalar2=None, op0=ALU.add)
            DVE("reciprocal", ["bg"], ["bg"], out=bg[:n, 0:4], in_=bg[:n, 0:4])
            DVE("tensor_tensor", ["ps0", "rb"], ["bg"], out=bg[:n, 4:8], in0=psb[0][:n, 4:8], in1=rb[:n, 4:8], op=ALU.add)
            ACT("activation", ["bg"], ["bg"], out=bg[:n, 4:8], in_=bg[:n, 4:8], func=AF.Exp)
            ACT("activation", ["bg"], ["bg"], out=bg[:n, 4:8], in_=bg[:n, 4:8], func=AF.Ln, bias=1.0)
            DVE("tensor_tensor", ["bg", "rb"], ["bg"], out=bg[:n, 4:8], in0=bg[:n, 4:8], in1=rb[:n, 0:4], op=ALU.mult)
            PE("matmul", ["bg", "triu"], ["ps0"], psb[0][:n, 8:12], lhsT=triu[:n, :n], rhs=bg[:n, 4:8], start=True, stop=True)
            DVE("tensor_copy", ["ps0"], ["gcum"], out=gcum[:n, :], in_=psb[0][:n, 8:12])
            DVE("tensor_copy", ["bg"], ["gb"], out=gb[:n, :, :], in_=bc(bg[:n, 4:8].unsqueeze(2), [n, 4, 128]))
            for hh in range(4):
                PE("matmul", ["gb", "triu"], ["ps1"], psb[1][:, hh * 128:hh * 128 + n], lhsT=gb[:n, hh, :], rhs=triu[:n, :n],
                   start=True, stop=True)
            for hh in range(4):
                DVE("scalar_tensor_tensor", ["ps1", "gcum", "mincT"], ["DTi"], out=DTi[:n, hh, 0:n],
                    in0=psb[1][:n, hh * 128:hh * 128 + n], scalar=gcum[:n, hh:hh + 1], in1=mincT[:n, :n],
                    op0=ALU.subtract, op1=ALU.add)
                DVE("scalar_tensor_tensor", ["ps1", "gcum", "pstr"], ["Dst"], out=Dst[:n, hh, 0:n],
                    in0=psb[1][:n, hh * 128:hh * 128 + n], scalar=gcum[:n, hh:hh + 1], in1=pstr[:n, :n],
                    op0=ALU.subtract, op1=ALU.add)
            ACT("activation", ["DTi"], ["DTi"], out=DTi[:n, :, 0:n], in_=DTi[:n, :, 0:n], func=AF.Exp)
            ACT("activation", ["Dst"], ["Dst"], out=Dst[:n, :, 0:n], in_=Dst[:n, :, 0:n], func=AF.Exp, scale=-1.0)
            pv = v3(psb[1][:, :], 4)
            ACT("activation", ["ps1"], ["egl"], out=egl[:, :], in_=pv[:, :, n - 1], func=AF.Exp)
            DVE("tensor_tensor", ["ps1", "gcum"], ["kds"], out=kds[:n, :], in0=pv[:n, :, n - 1], in1=gcum[:n, :], op=ALU.subtract)
            ACT("activation", ["kds"], ["kds"], out=kds[:n, :], in_=kds[:n, :], func=AF.Exp)
            ACT("activation", ["gcum"], ["egc"], out=egc[:n, :], in_=gcum[:n, :], func=AF.Exp)
            DVE("tensor_tensor", ["bg", "egc"], ["bge"], out=bge[:n, :], in0=bg[:n, 0:4], in1=egc[:n, :], op=ALU.mult)
            ACT("activation", ["ps1"], ["erow"], out=erow[:, :, 0:n], in_=ps3(1, n), func=AF.Exp)
            DVE("tensor_tensor", ["qT", "erow"], ["qdT"], out=qdT[:, :, 0:n], in0=qT[:, :, 0:n], in1=erow[:, :, 0:n], op=ALU.mult)
            DVE("tensor_tensor", ["ps6", "bg"], ["vb"], out=vb[:n, :, :], in0=v3(psb[6][:n, :], 4),
                in1=bc(bg[:n, 0:4].unsqueeze(2), [n, 4, 128]), op=ALU.mult)
            DVE("tensor_tensor", ["ps5", "bge"], ["kbg"], out=kbg[:n, :, :], in0=v3(psb[5][:n, :], 4),
                in1=bc(bge[:n, :].unsqueeze(2), [n, 4, 128]), op=ALU.mult)
            DVE("tensor_tensor", ["ps5", "kds"], ["kdec"], out=kdec[:n, :, :], in0=v3(psb[5][:n, :], 4),
                in1=bc(kds[:n, :].unsqueeze(2), [n, 4, 128]), op=ALU.mult)
            for hh in range(4):
                pk = f"ps{hh}"
                PE("matmul", ["qkv"], [pk], psb[hh][:n, 0:n], lhsT=qkv[:, 4 + hh, 0:n], rhs=qkv[:, 4 + hh, 0:n],
                   start=True, stop=True)
                PE("matmul", ["kT", "qT"], [pk], psb[hh][:n, 128:128 + n], lhsT=kT[:, hh, 0:n], rhs=qT[:, hh, 0:n],
                   start=True, stop=True)
            for hh in range(4):
                pk = f"ps{hh}"
                DVE("scalar_tensor_tensor", [pk, "bg", "Dst"], [f"X{hh}"], out=Xb[hh][:n, :n], in0=psb[hh][:n, 0:n],
                    scalar=bg[:n, hh:hh + 1], in1=Dst[:n, hh, 0:n], op0=ALU.mult, op1=ALU.mult)
                DVE("tensor_tensor", [pk, "DTi"], [f"attnT{hh}"], out=attnTh[hh][:n, :n], in0=psb[hh][:n, 128:128 + n],
                    in1=DTi[:n, hh, 0:n], op=ALU.mult)
            for hh in range(4):
                PE("matmul", [f"X{hh}", "ident"], [f"ps{hh}"], psb[hh][:n, 256:256 + n], lhsT=Xb[hh][:n, :n], rhs=ident[:n, :n],
                   start=True, stop=True)
            for hh in range(4):
                pk = f"ps{hh}"
                ACT("copy", [pk], [f"XT{hh}"], out=XTb[hh][:n, :n], in_=psb[hh][:n, 256:256 + n])
                DVE("tensor_tensor", ["ident", pk], [f"PT{hh}"], out=PTh[hh][:n, :n], in0=ident[:n, :n],
                    in1=psb[hh][:n, 256:256 + n], op=ALU.subtract)

        def gdn_X1(i):
            n = tile_n(i)
            nlev = int(round(math.log2(n))) - 1
            F32R = None
            for lev in range(nlev):
                last = (lev == nlev - 1)
                for hh in range(4):
                    pk = f"ps{hh}"
                    PE("matmul", [f"X{hh}", f"XT{hh}"], [pk], psb[hh][:n, 0:n], lhsT=XTb[hh][:n, :n], rhs=Xb[hh][:n, :n],
                       start=True, stop=True)
                for hh in range(4):
                    ACT("copy", [f"ps{hh}"], [f"X{hh}"], out=Xb[hh][:n, :n], in_=psb[hh][:n, 0:n])
                if not last:
                    for hh in range(4):
                        PE("transpose", [f"X{hh}", "ident"], [f"ps{hh}"], out=psb[hh][:n, 128:128 + n], in_=Xb[hh][:n, :n],
                           identity=ident[:n, :n])
                    for hh in range(4):
                        DVE("tensor_copy", [f"ps{hh}"], [f"XT{hh}"], out=XTb[hh][:n, :n], in_=psb[hh][:n, 128:128 + n])
                for hh in range(4):
                    b = 4 + hh % 2
                    PE("matmul", [f"X{hh}", f"PT{hh}"], [f"ps{b}"], psb[b][:n, (hh // 2) * 128:(hh // 2) * 128 + n],
                       lhsT=Xb[hh][:n, :n], rhs=PTh[hh][:n, :n], start=True, stop=True)
                for hh in range(4):
                    b = 4 + hh % 2
                    DVE("tensor_tensor", [f"ps{b}", f"PT{hh}"], [f"PT{hh}"], out=PTh[hh][:n, :n],
                        in0=psb[b][:n, (hh // 2) * 128:(hh // 2) * 128 + n], in1=PTh[hh][:n, :n], op=ALU.add)
            for hh in range(4):
                pk = f"ps{hh}"
                TT, TTk = PTh[hh], f"PT{hh}"
                PE("matmul", [TTk, "vb"], [pk], psb[hh][:n, 0:128], lhsT=TT[:n, :n], rhs=vb[:n, hh, :], start=True, stop=True)
                PE("matmul", [TTk, "kbg"], [pk], psb[hh][:, 128:128 + n], lhsT=kbg[:n, hh, :], rhs=TT[:n, :n],
                   start=True, stop=True)
            for hh in range(4):
                pk = f"ps{hh}"
                ACT("copy", [pk], [f"wT{hh}"], out=wTh[hh][:, 0:n], in_=psb[hh][:, 128:128 + n])
                DVE("tensor_copy", [pk], [f"u_s{hh}"], out=u_sh[hh][:n, :], in_=psb[hh][:n, 0:128])
            for hh in range(4):
                PE("matmul", [f"wT{hh}", "Sb"], [f"ps{hh}"], psb[hh][:n, 256:384], lhsT=wTh[hh][:, 0:n], rhs=Sb[:, hh, :],
                   start=True, stop=True)
            for hh in range(4):
                DVE("tensor_tensor", [f"u_s{hh}", f"ps{hh}"], [f"vnew{hh}"], out=vnewh[hh][:n, :], in0=u_sh[hh][:n, :],
                    in1=psb[hh][:n, 256:384], op=ALU.subtract)
            for hh in range(4):
                PE("matmul", ["qdT", "Sb"], ["ps7"], psb[7][:n, hh * 128:(hh + 1) * 128], lhsT=qdT[:, hh, 0:n], rhs=Sb[:, hh, :],
                   start=True, stop=False)
                PE("matmul", [f"attnT{hh}", f"vnew{hh}"], ["ps7"], psb[7][:n, hh * 128:(hh + 1) * 128], lhsT=attnTh[hh][:n, :n],
                   rhs=vnewh[hh][:n, :], start=False, stop=True)
                PE("matmul", ["kdec", f"vnew{hh}"], [f"ps{hh}"], psb[hh][:, 384:512], lhsT=kdec[:n, hh, :], rhs=vnewh[hh][:n, :],
                   start=True, stop=True)
            for hh in range(4):
                DVE("scalar_tensor_tensor", ["Sf", "egl", f"ps{hh}"], ["Sf"], out=Sf[:, hh, :], in0=Sf[:, hh, :],
                    scalar=egl[:, hh:hh + 1], in1=psb[hh][:, 384:512], op0=ALU.mult, op1=ALU.add)
                ACT("copy", ["Sf"], ["Sb"], out=Sb[:, hh, :], in_=Sf[:, hh, :])

        def gdn_X2(i):
            n = tile_n(i)
            nlev = int(round(math.log2(n))) - 1
            F32R = None
            for hh in range(4):
                ACT("activation", ["ps7"], ["junk", "ss4"], out=junk[:n, 0:128], in_=psb[7][:n, hh * 128:(hh + 1) * 128],
                    func=AF.Square, accum_out=ss4[:n, hh:hh + 1])
            rsqrt_inplace(ss4[:n, 0:4], "ss4", 1.0 / 128, EPS)
            for hh in range(4):
                DVE("scalar_tensor_tensor", ["ps7", "ss4", "rb"], ["o"], out=o[:n, hh * 128:(hh + 1) * 128],
                    in0=psb[7][:n, hh * 128:(hh + 1) * 128], scalar=ss4[:n, hh:hh + 1], in1=rb[:n, 8:136],
                    op0=ALU.mult, op1=ALU.mult)
            DVE("tensor_tensor", ["o", "zs"], ["o"], out=o[:n, :], in0=o[:n, :], in1=zs[:n, :], op=ALU.mult)
            tm_to_ynT(i, o, "o", ynT)
            outproj(i, ynT, wo)

        gdn_Y(0)
        for i in range(NT):
            gdn_pre(i)
            P.capture = []
            gdn_X1(i)
            A_ = P.capture
            P.capture = []
            if i + 1 < NT:
                gdn_Y(i + 1)
            B_ = P.capture
            P.capture = None
            P.replay_merged(A_, B_)
            gdn_X2(i)

    order = ["ffn1", "lru", "s5", "ssd", "gdn", "ffn2"]
    done = False
    for l in range(nlayers):
        if l > 0:
            P.new_epoch()
        for ph in order:
            if done:
                break
            if ph == "ffn1":
                ffn_alloc(W["ffn1_w_gate"][l], W["ffn1_w_up"][l], W["ffn1_w_down"][l])
                norm_phase(W["ffn1_norm"][l])
                ffn_phase(W["ffn1_w_gate"][l], W["ffn1_w_up"][l], W["ffn1_w_down"][l])
                P.barrier()
                lru_load(l)
                norm_phase(W["mix_norm"][l])
            elif ph == "lru":
                lru_phase(l)
            elif ph == "s5":
                s5_phase(l)
            elif ph == "ssd":
                ssd_phase(l)
            elif ph == "gdn":
                gdn_phase(l)
            elif ph == "ffn2":
                P.barrier()
                ffn_alloc(W["ffn2_w_gate"][l], W["ffn2_w_up"][l], W["ffn2_w_down"][l])
                norm_phase(W["ffn2_norm"][l])
                ffn_phase(W["ffn2_w_gate"][l], W["ffn2_w_up"][l], W["ffn2_w_down"][l])
            if stop_after == (l, ph):
                done = True

    P.barrier()
    if dbg == "h":
        P.dma("sp", DBGd[0:16, :], h[0:16, 0, :], "D_dbg", reads=["h:0"])
        dv = DBGd[16:T, :].rearrange("(i p) d -> p i d", p=128)
        for i in range(1, NT):
            P.dma("sp", dv[:, i - 1, :], h[:, i, :], "D_dbg", reads=[f"h:{i}"])
        P.wait_sem("sp", "D_dbg")

    ov = OUTd.rearrange("(i p) d -> p i d", p=128)
    if final:
        WA.reset()
        grow = WA.get([128, D])
        fv = W["final_norm"]
        P.dma("sp", grow, AP(fv.tensor, fv.offset, [[0, 128], [1, D]]), "D_grow", writes=["grow"])
        for i in range(1, NT):
            ACT("activation", [f"h:{i}"], ["junk", "ss"], out=junk[:, :], in_=h[:, i, :], func=AF.Square,
                accum_out=ss[:, i:i + 1])
        rsqrt_inplace(ss[:, :], "ss", 1.0 / D, EPS)
        for i in range(1, NT):
            s_ = i % 2
            DVE("scalar_tensor_tensor", [f"h:{i}", "ss", "grow"], [f"xs:{s_}"], out=xs[:, s_, :], in0=h[:, i, :],
                scalar=ss[:, i:i + 1], in1=grow, op0=ALU.mult, op1=ALU.mult)
            P.dma("sp", ov[:, i - 1, :], xs[:, s_, :], f"D_out{s_}", reads=[f"xs:{s_}"])
        P.wait_sem("sp", "D_out0")
        P.wait_sem("sp", "D_out1")
    else:
        for i in range(1, NT):
            P.dma("sp", ov[:, i - 1, :], h[:, i, :], "D_out", reads=[f"h:{i}"])
        P.wait_sem("sp", "D_out")

    if dbg:
        print('op counts', P.cnt, {k_: v_ for k_, v_ in P.dcnt.items() if v_ > 2000})
    P.emit()
    es.close()
    return nc


_NC_CACHE = {}


def kernel(**inputs):
    nc = _NC_CACHE.get("nc")
    if nc is None:
        nc = build()
        _NC_CACHE["nc"] = nc
    x = np.ascontiguousarray(inputs["x"], dtype=np.float32)
    in_maps = []
    for c in range(NCORES):
        m = {"x": x[c]}
        for kname, v in inputs.items():
            if kname != "x":
                m[kname] = np.ascontiguousarray(v, dtype=np.float32)
        in_maps.append(m)
    res = run_bass_kernel_spmd(nc, in_maps, core_ids=list(range(NCORES)))
    return np.stack([r["out"] for r in res.results], axis=0)
```

```python
from contextlib import ExitStack
import math
import numpy as np
import concourse.bass as bass
import concourse.mybir as mybir
from concourse.ap import AP
from concourse.bass_utils import run_bass_kernel_spmd

F32 = mybir.dt.float32
BF16 = mybir.dt.bfloat16
AF = mybir.ActivationFunctionType
ALU = mybir.AluOpType
AX = mybir.AxisListType

D = 1024
DFF = 2816
T = 2064
NT = 17
DIN = 5136
EPS = 1e-6
NCORES = 8
NEG = -30000.0


def tile_n(i):
    return 16 if i == 0 else 128


def tile_t0(i):
    return 0 if i == 0 else 16 + (i - 1) * 128


class Prog:
    ENG = ("pe", "act", "dve", "pool", "sp")

    def __init__(self, nc, es):
        self.nc = nc
        self.es = es
        self.ops = {e: [] for e in self.ENG}
        self.cnt = {e: 0 for e in self.ENG}
        self.sems = {}
        self.dcnt = {}
        self.waited = {}
        self.last_w = {}
        self.readers = {}
        self.epoch = 0
        self.cur = {}
        for e in self.ENG:
            self.cur[e] = "E_" + e
            self.sems["E_" + e] = es.enter_context(nc.semaphore("E_" + e))

    def _deps(self, eng, reads, writes):
        deps = {}

        def add(tok):
            if tok is None:
                return
            s, v = tok
            if eng == "pe" and s.startswith("E_pe"):
                return
            if deps.get(s, 0) < v:
                deps[s] = v

        own = self.cur[eng] if eng in self.cur else None
        for k in reads:
            add(self.last_w.get(k))
        rset = set(reads)
        for k in writes:
            lw = self.last_w.get(k)
            if lw is not None and not (lw[0] == own and k not in rset):
                add(lw)
            for tok in self.readers.get(k, ()):
                if tok[0] != own:
                    add(tok)
        waits = []
        for s, v in deps.items():
            if self.waited.get((eng, s), 0) < v:
                self.waited[(eng, s)] = v
                waits.append((self.sems[s], v))
        return waits

    def _record(self, tok, reads, writes):
        for k in writes:
            self.last_w[k] = tok
            self.readers[k] = []
        for k in reads:
            if k in writes:
                continue
            self.readers.setdefault(k, []).append(tok)

    capture = None

    def replay_merged(self, A, B):
        ia = ib = 0
        na, nb = len(A), len(B)
        while ia < na or ib < nb:
            if ib >= nb or (ia < na and ia * nb <= ib * na):
                kind, a, kw = A[ia]
                ia += 1
            else:
                kind, a, kw = B[ib]
                ib += 1
            if kind == "op":
                self.op(*a, **kw)
            else:
                self.dma(*a, **kw)

    def op(self, eng, name, reads, writes, *args, **kw):
        if self.capture is not None:
            self.capture.append(("op", (eng, name, reads, writes) + args, kw))
            return None
        pk = [k for k in reads if k.startswith("ps") and k[2:].isdigit()]
        if pk:
            writes = list(writes) + [k for k in pk if k not in writes]
        waits = self._deps(eng, reads, writes)
        self.cnt[eng] += 1
        tok = (self.cur[eng], self.cnt[eng])
        self.ops[eng].append((waits, name, args, kw, self.sems[self.cur[eng]], 1))
        self._record(tok, reads, writes)
        return tok

    def dma(self, q, out, in_, sem, reads=(), writes=(), **kw):
        if self.capture is not None:
            kw2 = dict(kw)
            kw2["reads"] = reads
            kw2["writes"] = writes
            self.capture.append(("dma", (q, out, in_, sem), kw2))
            return None
        if sem not in self.sems:
            self.sems[sem] = self.es.enter_context(self.nc.semaphore(sem))
            self.dcnt[sem] = 0
        waits = self._deps(q, reads, writes)
        prev = self.dcnt[sem]
        if prev > 0 and self.waited.get((q, sem), 0) < prev:
            self.waited[(q, sem)] = prev
            waits.append((self.sems[sem], prev))
        self.dcnt[sem] = prev + 16
        tok = (sem, prev + 16)
        kw = dict(kw)
        kw["out"] = out
        kw["in_"] = in_
        self.ops[q].append((waits, "dma_start", (), kw, self.sems[sem], 16))
        self._record(tok, reads, writes)
        return tok

    def barrier(self):
        toks = [(self.cur[e], self.cnt[e]) for e in self.ENG if self.cnt[e] > 0]
        toks += [(s, c) for s, c in self.dcnt.items() if c > 0]
        for e in self.ENG:
            waits = []
            for s, v in toks:
                if s == self.cur[e]:
                    continue
                if self.waited.get((e, s), 0) < v:
                    self.waited[(e, s)] = v
                    waits.append((self.sems[s], v))
            self.ops[e].append((waits, None, None, None, None, 0))
        self.last_w = {}
        self.readers = {}

    def new_epoch(self):
        self.barrier()
        self.epoch += 1
        for e in self.ENG:
            nm = f"E_{e}_{self.epoch}"
            self.sems[nm] = self.es.enter_context(self.nc.semaphore(nm))
            self.cur[e] = nm
            self.cnt[e] = 0

    def wait_sem(self, eng, sem):
        self.ops[eng].append(([(self.sems[sem], self.dcnt[sem])], None, None, None, None, 0))

    def emit(self):
        nc = self.nc
        with nc.Block() as block:
            def mk(name):
                lst = self.ops[name]

                def body(eng):
                    for waits, nm, args, kw, sem, inc in lst:
                        for s, v in waits:
                            eng.wait_ge(s, v)
                        if nm is None:
                            continue
                        ins = getattr(eng, nm)(*args, **kw)
                        ins.then_inc(sem, inc)
                return body

            block.tensor(mk("pe"))
            block.scalar(mk("act"))
            block.vector(mk("dve"))
            block.gpsimd(mk("pool"))
            block.sync(mk("sp"))


class Arena:
    def __init__(self, tens, words):
        self.t = tens
        self.words = words
        self.off = 0

    def reset(self):
        self.off = 0

    def get(self, shape, dt=F32):
        nel = 1
        for s in shape[1:]:
            nel *= s
        w = nel if dt == F32 else (nel + 1) // 2
        w = (w + 1) // 2 * 2
        v = self.t[:, self.off:self.off + w]
        self.off += w
        assert self.off <= self.words, (self.off, self.words)
        if dt == BF16:
            v = v.bitcast(BF16)[:, 0:nel]
        if len(shape) == 3:
            v = v.rearrange("p (a b) -> p a b", a=shape[1])
        elif len(shape) == 4:
            v = v.rearrange("p (a b c) -> p a b c", a=shape[1], b=shape[2])
        return v


def bc(ap, shape):
    return ap.to_broadcast(list(shape))


def build(nlayers=4, final=True, dbg=None, stop_after=None):
    nc = bass.Bass("TRN2", target_bir_lowering=False)
    es = ExitStack()
    P = Prog(nc, es)
    es.enter_context(nc.allow_non_contiguous_dma(reason="small strided parameter loads"))

    def dram(name, shape, kind="ExternalInput"):
        return nc.dram_tensor(name, list(shape), F32, kind=kind).ap()

    L = nlayers
    Xd = dram("x", [2048, D])
    METAd = dram("meta_tokens", [16, D])
    shapes = {
        "ffn1_norm": [L, D], "ffn1_w_gate": [L, D, DFF], "ffn1_w_up": [L, D, DFF], "ffn1_w_down": [L, DFF, D],
        "mix_norm": [L, D], "w_in": [L, D, DIN], "w_out": [L, 2048, D],
        "lru_conv_w": [L, 4, 512], "lru_conv_b": [L, 512], "lru_w_a": [L, 8, 64, 64], "lru_b_a": [L, 512],
        "lru_w_i": [L, 8, 64, 64], "lru_b_i": [L, 512], "lru_lambda": [L, 512], "lru_norm": [L, 512],
        "gdn_conv_w": [L, 4, 1536], "gdn_a_log": [L, 4], "gdn_dt_bias": [L, 4], "gdn_norm": [L, 128],
        "ssd_conv_w": [L, 4, 1024], "ssd_conv_b": [L, 1024], "ssd_a_log": [L, 8], "ssd_dt_bias": [L, 8],
        "ssd_d": [L, 8], "ssd_norm": [L, 512],
        "s5_a_re": [L, 32, 64], "s5_a_im": [L, 32, 64], "s5_log_dt": [L, 32],
        "s5_b_re": [L, 32, 64, 16], "s5_b_im": [L, 32, 64, 16], "s5_c_re": [L, 32, 16, 64], "s5_c_im": [L, 32, 16, 64],
        "s5_d": [L, 512], "s5_w_glu": [L, 512, 512], "s5_norm": [L, 512],
        "ffn2_norm": [L, D], "ffn2_w_gate": [L, D, DFF], "ffn2_w_up": [L, D, DFF], "ffn2_w_down": [L, DFF, D],
        "final_norm": [D],
    }
    W = {n: dram(n, s) for n, s in shapes.items()}
    OUTd = dram("out", [2048, D], kind="ExternalOutput")
    if dbg:
        DBGd = dram("dbg", [T, D], kind="ExternalOutput")

    def sb(name, shape, dt=F32):
        return es.enter_context(nc.sbuf_tensor(name, list(shape), dt))

    h = sb("h", [128, NT, D])
    xnT = sb("xnT", [128, 8, T], BF16)
    ident = sb("ident", [128, 128])
    ones = sb("ones", [128, 128])
    triu = sb("triu", [128, 128])
    mincT = sb("mincT", [128, 128])
    mstrT = sb("mstrT", [128, 128])
    pstr = sb("pstr", [128, 128])
    gmask = sb("gmask", [128, 8])
    psb = [es.enter_context(nc.psum_tensor(f"ps{i}", [128, 512], F32)) for i in range(8)]
    junk = sb("junk", [128, D], BF16)
    xs = sb("xs", [128, 2, D])
    ss = sb("ss", [128, NT])
    rstd = sb("rstd", [128, NT])
    prow = sb("prow", [128, 128])
    pc = sb("pc", [128, 128])
    WA_t = sb("WA", [128, 11520])
    MA_t = sb("MA", [128, 12288])
    WA = Arena(WA_t, 11520)
    MA = Arena(MA_t, 12288)

    def PE(name, r, w, *a, **kw):
        return P.op("pe", name, r, w, *a, **kw)

    def ACT(name, r, w, *a, **kw):
        return P.op("act", name, r, w, *a, **kw)

    def DVE(name, r, w, *a, **kw):
        return P.op("dve", name, r, w, *a, **kw)

    def POOL(name, r, w, *a, **kw):
        return P.op("pool", name, r, w, *a, **kw)

    POOL("memset", [], ["ident"], ident[:], 0.0)
    POOL("affine_select", ["ident"], ["ident"], out=ident[:], in_=ident[:], pattern=[[-1, 128]],
         compare_op=ALU.not_equal, fill=1.0, base=0, channel_multiplier=1)
    POOL("memset", [], ["ones"], ones[:], 1.0)
    POOL("memset", [], ["triu"], triu[:], 1.0)
    POOL("affine_select", ["triu"], ["triu"], out=triu[:], in_=triu[:], pattern=[[1, 128]],
         compare_op=ALU.is_ge, fill=0.0, base=0, channel_multiplier=-1)
    POOL("memset", [], ["mincT"], mincT[:], 0.0)
    POOL("affine_select", ["mincT"], ["mincT"], out=mincT[:], in_=mincT[:], pattern=[[1, 128]],
         compare_op=ALU.is_ge, fill=NEG, base=0, channel_multiplier=-1)
    POOL("memset", [], ["mstrT"], mstrT[:], 0.0)
    POOL("affine_select", ["mstrT"], ["mstrT"], out=mstrT[:], in_=mstrT[:], pattern=[[1, 128]],
         compare_op=ALU.is_gt, fill=NEG, base=0, channel_multiplier=-1)
    POOL("memset", [], ["pstr"], pstr[:], 0.0)
    POOL("affine_select", ["pstr"], ["pstr"], out=pstr[:], in_=pstr[:], pattern=[[-1, 128]],
         compare_op=ALU.is_gt, fill=-NEG, base=0, channel_multiplier=1)
    DVE("tensor_reduce", ["ident"], ["gmask"], out=gmask[:], in_=ident[:].rearrange("p (g i) -> p g i", g=8),
        axis=AX.X, op=ALU.add)
    POOL("memset", [], ["ss"], ss[:], 1.0)

    P.dma("sp", h[0:16, 0, :], METAd[:, :], "D_h0", writes=["h:0"])
    xv = Xd.rearrange("(i p) d -> p i d", p=128)
    for i in range(1, NT):
        P.dma("sp", h[:, i, :], xv[:, i - 1, :], f"D_h{i}", writes=[f"h:{i}"])

    def ps3(b, n):
        return psb[b][:, :].rearrange("p (c t) -> p c t", c=4)[:, :, 0:n]

    def load_cols(srcs):
        r0 = 0
        for s in srcs:
            r = s.shape[0]
            P.dma("sp", prow[r0:r0 + r, :], s, "D_prow", writes=["prow"])
            r0 += r
        assert r0 <= 128
        PE("transpose", ["prow", "ident"], ["ps7"], out=psb[7][:, 0:r0], in_=prow[0:r0, :], identity=ident[0:r0, 0:r0])
        DVE("tensor_copy", ["ps7"], ["pc"], out=pc[:, 0:r0], in_=psb[7][:, 0:r0])

    def rsqrt_inplace(ap, key, scale, eps):
        DVE("tensor_scalar", [key], [key], out=ap, in0=ap, scalar1=scale, scalar2=eps, op0=ALU.mult, op1=ALU.add)
        ACT("activation", [key], [key], out=ap, in_=ap, func=AF.Ln)
        ACT("activation", [key], [key], out=ap, in_=ap, func=AF.Exp, scale=-0.5)

    def norm_phase(gvec):
        load_cols([gvec.rearrange("(k p) -> k p", p=128)])
        for i in range(NT):
            n = tile_n(i)
            ACT("activation", [f"h:{i}"], ["junk", "ss"], out=junk[:n, :], in_=h[:n, i, :], func=AF.Square,
                accum_out=ss[:n, i:i + 1])
        rsqrt_inplace(ss[:, :], "ss", 1.0 / D, EPS)
        for i in range(NT):
            n = tile_n(i)
            t0 = tile_t0(i)
            s = i % 2
            ACT("activation", [f"h:{i}", "ss"], [f"xs:{s}"], out=xs[:n, s, :], in_=h[:n, i, :], func=AF.Copy,
                scale=ss[:n, i:i + 1])
            for half in range(2):
                b = (i * 2 + half) % 4
                for q in range(4):
                    kc = half * 4 + q
                    PE("transpose", [f"xs:{s}", "ident"], [f"ps{b}"], out=psb[b][:, q * 128:q * 128 + n],
                       in_=xs[:n, s, kc * 128:(kc + 1) * 128], identity=ident[:n, :n])
                DVE("tensor_tensor", [f"ps{b}", "pc"], [f"xnT:{i}"], out=xnT[:, half * 4:half * 4 + 4, t0:t0 + n],
                    in0=ps3(b, n), in1=bc(pc[:, half * 4:half * 4 + 4].unsqueeze(2), [128, 4, n]), op=ALU.mult)

    FS = 256
    NS = DFF // FS
    blocks = [[0], [1, 2, 3, 4], [5, 6, 7, 8], [9, 10, 11, 12], [13, 14, 15, 16]]
    st = {"ffn_it": 0}

    def ffn_alloc(wgate, wup, wdown):
        WA.reset()
        wg = [WA.get([128, 8, FS], BF16) for _ in range(2)]
        wu = [WA.get([128, 8, FS], BF16) for _ in range(2)]
        wd = [WA.get([128, 2, D], BF16) for _ in range(2)]
        sg = [WA.get([128, 2, 512]) for _ in range(2)]
        actT = [WA.get([128, 2, 512], BF16) for _ in range(2)]
        wgv = wgate.rearrange("(k p) f -> p k f", p=128)
        wuv = wup.rearrange("(k p) f -> p k f", p=128)
        wdv = wdown.rearrange("(c p) d -> p c d", p=128)

        def load(sl):
            j = sl % 2
            P.dma("pool", wg[j], wgv[:, :, sl * FS:(sl + 1) * FS], f"D_wg{j}", writes=[f"wg{j}"])
            P.dma("pool", wu[j], wuv[:, :, sl * FS:(sl + 1) * FS], f"D_wu{j}", writes=[f"wu{j}"])
            P.dma("pool", wd[j], wdv[:, sl * 2:sl * 2 + 2, :], f"D_wd{j}", writes=[f"wd{j}"])

        load(0)
        st["ffn"] = (wg, wu, wd, sg, actT, load)

    def ffn_phase(wgate, wup, wdown):
        wg, wu, wd, sg, actT, load = st["ffn"]
        for sl in range(NS):
            j = sl % 2
            if sl + 1 < NS:
                load(sl + 1)
            for blk in blocks:
                a = st["ffn_it"] % 2
                st["ffn_it"] += 1
                t0 = tile_t0(blk[0])
                nt = sum(tile_n(i) for i in blk)
                rk = [f"xnT:{i}" for i in blk]
                for (wsb, wkey, pbase) in ((wg[j], f"wg{j}", 0), (wu[j], f"wu{j}", 2)):
                    for fc in range(2):
                        for kc in range(8):
                            PE("matmul", [wkey] + rk, [f"ps{pbase + fc}"], psb[pbase + fc][:, 0:nt],
                               lhsT=wsb[:, kc, fc * 128:(fc + 1) * 128], rhs=xnT[:, kc, t0:t0 + nt],
                               start=(kc == 0), stop=(kc == 7))
                for fc in range(2):
                    ACT("activation", [f"ps{fc}"], [f"sg{a}:{fc}"], out=sg[a][:, fc, 0:nt], in_=psb[fc][:, 0:nt], func=AF.Silu)
                    DVE("tensor_tensor", [f"sg{a}:{fc}", f"ps{2 + fc}"], [f"actT{a}:{fc}"], out=actT[a][:, fc, 0:nt],
                        in0=sg[a][:, fc, 0:nt], in1=psb[2 + fc][:, 0:nt], op=ALU.mult)
                off = 0
                for ti, i in enumerate(blk):
                    n = tile_n(i)
                    for dh in range(2):
                        pbi = 4 + ((ti * 2 + dh) % 4)
                        for fc in range(2):
                            PE("matmul", [f"actT{a}:{fc}", f"wd{j}"], [f"ps{pbi}"], psb[pbi][:n, :],
                               lhsT=actT[a][:, fc, off:off + n], rhs=wd[j][:, fc, dh * 512:(dh + 1) * 512],
                               start=(fc == 0), stop=(fc == 1))
                        DVE("scalar_tensor_tensor", [f"ps{pbi}", f"h:{i}"], [f"h:{i}"], out=h[:n, i, dh * 512:(dh + 1) * 512],
                            in0=psb[pbi][:n, :], scalar=0.5, in1=h[:n, i, dh * 512:(dh + 1) * 512], op0=ALU.mult, op1=ALU.add)
                    off += n

    def load_win(wbig, l, c0, c1):
        wv = W["w_in"][l].rearrange("(k p) c -> p k c", p=128)
        c = c0
        while c < c1:
            ce = min(c + 1024, c1)
            P.dma("pool", wbig[:, :, c - c0:ce - c0], wv[:, :, c:ce], "D_win", writes=["wbig"])
            c = ce

    def load_wout(wo, l, r0):
        wv = W["w_out"][l].rearrange("(k p) d -> p k d", p=128)
        P.dma("pool", wo, wv[:, r0 // 128:r0 // 128 + 4, :], "D_wo", writes=["wo"])

    def inproj_fm(i, wbig, col0, nch, banks):
        n, t0 = tile_n(i), tile_t0(i)
        for c in range(nch):
            b = banks[c // 4]
            for kc in range(8):
                PE("matmul", ["wbig", f"xnT:{i}"], [f"ps{b}"], psb[b][:, (c % 4) * 128:(c % 4) * 128 + n],
                   lhsT=wbig[:, kc, col0 + c * 128:col0 + (c + 1) * 128], rhs=xnT[:, kc, t0:t0 + n],
                   start=(kc == 0), stop=(kc == 7))

    def inproj_tm(i, wbig, col0, ncol, b):
        n, t0 = tile_n(i), tile_t0(i)
        for kc in range(8):
            PE("matmul", ["wbig", f"xnT:{i}"], [f"ps{b}"], psb[b][:n, 0:ncol],
               lhsT=xnT[:, kc, t0:t0 + n], rhs=wbig[:, kc, col0:col0 + ncol], start=(kc == 0), stop=(kc == 7))

    def conv_fm(cb, ckey, dst, dkey, nch, n, wcol0, bcol0, tmpb, tkey):
        for c0 in range(0, nch, 4):
            cs = slice(c0, c0 + 4)

            def wb(col):
                return bc(pc[:, col + c0:col + c0 + 4].unsqueeze(2), [128, 4, n])

            DVE("tensor_tensor", [ckey, "pc"], [dkey], out=dst[:, cs, 0:n], in0=cb[:, cs, 0:n], in1=wb(wcol0), op=ALU.mult)
            for kk in range(1, 4):
                DVE("tensor_tensor", [ckey, "pc"], [tkey], out=tmpb[:, 0:4, 0:n], in0=cb[:, cs, kk:kk + n],
                    in1=wb(wcol0 + kk * nch), op=ALU.mult)
                DVE("tensor_tensor", [dkey, tkey], [dkey], out=dst[:, cs, 0:n], in0=dst[:, cs, 0:n], in1=tmpb[:, 0:4, 0:n],
                    op=ALU.add)
            if bcol0 is not None:
                DVE("tensor_tensor", [dkey, "pc"], [dkey], out=dst[:, cs, 0:n], in0=dst[:, cs, 0:n], in1=wb(bcol0), op=ALU.add)
        ACT("copy", [ckey], [ckey], out=cb[:, :, 0:3], in_=cb[:, :, n:n + 3])

    def gelu_tanh(dst, dkey, src, skey, tmp, tkey, shape_ap):
        ACT("activation", [skey], [tkey], out=tmp, in_=src, func=AF.Square)
        DVE("tensor_scalar", [tkey], [tkey], out=tmp, in0=tmp, scalar1=0.044715, scalar2=1.0, op0=ALU.mult, op1=ALU.add)
        DVE("tensor_tensor", [tkey, skey], [tkey], out=tmp, in0=tmp, in1=src, op=ALU.mult)
        ACT("activation", [tkey], [tkey], out=tmp, in_=tmp, func=AF.Sigmoid, scale=1.5957691216057308)
        DVE("tensor_tensor", [tkey, skey], [dkey], out=dst, in0=tmp, in1=src, op=ALU.mult)

    def chan_rmsnorm_fm(y, ykey, ynT, n, gcol0, sq, rs):
        ACT("activation", [ykey], ["sq"], out=sq[:, :, 0:n], in_=y[:, :, 0:n], func=AF.Square)
        for c in range(4):
            PE("matmul", ["sq", "ones"], ["ps4"], psb[4][:, 0:n], lhsT=ones[:, :], rhs=sq[:, c, 0:n],
               start=(c == 0), stop=(c == 3))
        DVE("tensor_scalar", ["ps4"], ["rs"], out=rs[:, 0:n], in0=psb[4][:, 0:n], scalar1=1.0 / 512, scalar2=EPS,
            op0=ALU.mult, op1=ALU.add)
        ACT("activation", ["rs"], ["rs"], out=rs[:, 0:n], in_=rs[:, 0:n], func=AF.Ln)
        ACT("activation", ["rs"], ["rs"], out=rs[:, 0:n], in_=rs[:, 0:n], func=AF.Exp, scale=-0.5)
        for c in range(4):
            DVE("scalar_tensor_tensor", [ykey, "pc", "rs"], ["ynT"], out=ynT[:, c, 0:n], in0=y[:, c, 0:n],
                scalar=pc[:, gcol0 + c:gcol0 + c + 1], in1=rs[:, 0:n], op0=ALU.mult, op1=ALU.mult)

    def outproj(i, ynT, wo):
        n = tile_n(i)
        for dh in range(2):
            b = 6 + dh
            for c in range(4):
                PE("matmul", ["ynT", "wo"], [f"ps{b}"], psb[b][:n, :], lhsT=ynT[:, c, 0:n],
                   rhs=wo[:, c, dh * 512:(dh + 1) * 512], start=(c == 0), stop=(c == 3))
            DVE("tensor_tensor", [f"ps{b}", f"h:{i}"], [f"h:{i}"], out=h[:n, i, dh * 512:(dh + 1) * 512],
                in0=psb[b][:n, :], in1=h[:n, i, dh * 512:(dh + 1) * 512], op=ALU.add)

    def tm_to_ynT(i, ytm, ykey, ynT):
        n = tile_n(i)
        for c in range(4):
            PE("transpose", [ykey, "ident"], ["ps5"], out=psb[5][:, c * 128:c * 128 + n],
               in_=ytm[:n, c * 128:(c + 1) * 128], identity=ident[:n, :n])
        ACT("copy", ["ps5"], ["ynT"], out=ynT[:, :, 0:n], in_=ps3(5, n))

    def lru_load(l):
        WA.reset()
        wbig = WA.get([128, 8, 1024], BF16)
        wo = WA.get([128, 4, D], BF16)
        wbd = WA.get([128, 2, 4, 128], BF16)
        load_win(wbig, l, 0, 1024)
        load_wout(wo, l, 0)
        POOL("memset", [], ["wbd"], wbd[:, :, :, :], 0.0)
        for wi, nm in enumerate(("lru_w_a", "lru_w_i")):
            for hh in range(8):
                e = hh % 2
                P.dma("pool", wbd[e * 64:(e + 1) * 64, wi, hh // 2, e * 64:(e + 1) * 64], W[nm][l, hh], "D_wbd",
                      reads=[], writes=["wbd"])
        s5w = (WA.get([128, 8, 512], BF16), WA.get([128, 4, D], BF16))
        wv = W["w_in"][l].rearrange("(k p) c -> p k c", p=128)
        P.dma("pool", s5w[0], wv[:, :, 4624:5136], "D_win5", writes=["wbig5"])
        wov = W["w_out"][l].rearrange("(k p) d -> p k d", p=128)
        P.dma("pool", s5w[1], wov[:, 12:16, :], "D_wo5", writes=["wo5"])
        st["lru_w"] = (wbig, wo, wbd)
        st["s5_w"] = s5w

    def lru_phase(l):
        P.barrier()
        MA.reset()
        wbig, wo, wbd = st["lru_w"]
        cb = MA.get([128, 4, 131])
        tmp = MA.get([128, 4, 128])
        gl2 = [MA.get([128, 4, 128]) for _ in range(2)]
        xc2 = [MA.get([128, 4, 128]) for _ in range(2)]
        xcb2 = [MA.get([128, 4, 128], BF16) for _ in range(2)]
        r = MA.get([128, 4, 128])
        ig = MA.get([128, 4, 128])
        aa = MA.get([128, 4, 128])
        mm = MA.get([128, 4, 128])
        hs = MA.get([128, 4, 128])
        sq = MA.get([128, 4, 128])
        rs = MA.get([128, 128])
        ynT = MA.get([128, 4, 128], BF16)
        carry = MA.get([128, 4])
        cl = MA.get([128, 8])
        load_cols([W["lru_conv_w"][l].rearrange("k (c p) -> (k c) p", p=128),
                   W["lru_conv_b"][l].rearrange("(c p) -> c p", p=128),
                   W["lru_b_a"][l].rearrange("(c p) -> c p", p=128),
                   W["lru_b_i"][l].rearrange("(c p) -> c p", p=128),
                   W["lru_lambda"][l].rearrange("(c p) -> c p", p=128),
                   W["lru_norm"][l].rearrange("(c p) -> c p", p=128)])
        ACT("activation", ["pc"], ["cl"], out=cl[:, 0:4], in_=pc[:, 28:32], func=AF.Exp, scale=-1.0)
        ACT("activation", ["cl"], ["cl"], out=cl[:, 0:4], in_=cl[:, 0:4], func=AF.Ln, bias=1.0)
        DVE("tensor_scalar", ["cl"], ["cl"], out=cl[:, 4:8], in0=cl[:, 0:4], scalar1=-16.0, scalar2=None, op0=ALU.mult)
        DVE("tensor_scalar", ["cl"], ["cl"], out=cl[:, 0:4], in0=cl[:, 0:4], scalar1=-8.0, scalar2=None, op0=ALU.mult)
        DVE("memset", [], ["cb"], cb[:, :, 0:3], 0.0)
        DVE("memset", [], ["carry"], carry[:, :], 0.0)
        def lru_front(i):
            n = tile_n(i)
            p = i % 2
            gl, xc, xcb = gl2[p], xc2[p], xcb2[p]
            inproj_fm(i, wbig, 0, 8, [0, 1])
            ACT("copy", ["ps0"], ["cb"], out=cb[:, :, 3:3 + n], in_=ps3(0, n))
            gelu_tanh(gl[:, :, 0:n], f"gl{p}", ps3(1, n), "ps1", tmp[:, :, 0:n], "tmp", None)
            conv_fm(cb, "cb", xc, f"xc{p}", 4, n, 0, 16, tmp, "tmp")
            ACT("copy", [f"xc{p}"], [f"xcb{p}"], out=xcb[:, :, 0:n], in_=xc[:, :, 0:n])

        def lru_back(i):
            n = tile_n(i)
            p = i % 2
            gl, xc, xcb = gl2[p], xc2[p], xcb2[p]
            for wi in range(2):
                for c in range(4):
                    PE("matmul", ["wbd", f"xcb{p}"], [f"ps{2 + wi}"], psb[2 + wi][:, c * 128:c * 128 + n],
                       lhsT=wbd[:, wi, c, :], rhs=xcb[:, c, 0:n], start=True, stop=True)
            for c in range(4):
                ACT("activation", ["ps2", "pc"], ["r"], out=r[:, c, 0:n], in_=psb[2][:, c * 128:c * 128 + n],
                    func=AF.Sigmoid, bias=pc[:, 20 + c:21 + c])
                ACT("activation", ["ps3", "pc"], ["ig"], out=ig[:, c, 0:n], in_=psb[3][:, c * 128:c * 128 + n],
                    func=AF.Sigmoid, bias=pc[:, 24 + c:25 + c])
            for c in range(4):
                ACT("activation", ["r", "cl"], ["aa"], out=aa[:, c, 0:n], in_=r[:, c, 0:n], func=AF.Exp, scale=cl[:, c:c + 1])
                ACT("activation", ["r", "cl"], ["mm"], out=mm[:, c, 0:n], in_=r[:, c, 0:n], func=AF.Exp, scale=cl[:, 4 + c:5 + c])
            DVE("tensor_scalar", ["mm"], ["mm"], out=mm[:, :, 0:n], in0=mm[:, :, 0:n], scalar1=-1.0, scalar2=1.0,
                op0=ALU.mult, op1=ALU.add)
            ACT("activation", ["mm"], ["mm"], out=mm[:, :, 0:n], in_=mm[:, :, 0:n], func=AF.Ln)
            ACT("activation", ["mm"], ["mm"], out=mm[:, :, 0:n], in_=mm[:, :, 0:n], func=AF.Exp, scale=0.5)
            DVE("tensor_tensor", ["ig", f"xc{p}"], ["ig"], out=ig[:, :, 0:n], in0=ig[:, :, 0:n], in1=xc[:, :, 0:n], op=ALU.mult)
            DVE("tensor_tensor", ["ig", "mm"], ["ig"], out=ig[:, :, 0:n], in0=ig[:, :, 0:n], in1=mm[:, :, 0:n], op=ALU.mult)
            for c in range(4):
                DVE("tensor_tensor_scan", ["aa", "ig", "carry"], ["hs"], out=hs[:, c, 0:n], data0=aa[:, c, 0:n],
                    data1=ig[:, c, 0:n], initial=carry[:, c:c + 1], op0=ALU.mult, op1=ALU.add)
            DVE("tensor_copy", ["hs"], ["carry"], out=carry[:, :], in_=hs[:, :, n - 1])
            DVE("tensor_tensor", ["hs", f"gl{p}"], ["hs"], out=hs[:, :, 0:n], in0=hs[:, :, 0:n], in1=gl[:, :, 0:n], op=ALU.mult)
            chan_rmsnorm_fm(hs, "hs", ynT, n, 32, sq, rs)
            outproj(i, ynT, wo)

        lru_front(0)
        for i in range(NT):
            P.capture = []
            lru_back(i)
            A = P.capture
            P.capture = []
            if i + 1 < NT:
                lru_front(i + 1)
            B = P.capture
            P.capture = None
            P.replay_merged(A, B)

    PI = math.pi

    def s5_phase(l):
        P.barrier()
        WA.reset()
        MA.reset()
        wbig, wo = st["s5_w"]
        BT = WA.get([128, 2, 16, 128], BF16)
        CT = WA.get([128, 2, 16, 128], BF16)
        wglu = WA.get([128, 4, 512], BF16)
        P.dma("pool", wglu, W["s5_w_glu"][l].rearrange("(k p) c -> p k c", p=128), "D_wglu", writes=["wglu"])
        load_cols([W["s5_d"][l].rearrange("(c p) -> c p", p=128), W["s5_norm"][l].rearrange("(c p) -> c p", p=128)])
        cosT = MA.get([128, 16, 128])
        sinT = MA.get([128, 16, 128])
        rho = MA.get([128, 16])
        Sre = MA.get([128, 16])
        Sim = MA.get([128, 16])
        mark = MA.off
        sm = {nm: MA.get([128, 16]) for nm in ("are", "aim", "ldt", "th", "c1", "s1", "fre", "fim", "a", "b", "c", "den")}
        bw = [MA.get([128, 16, 128]) for _ in range(2)]
        tt = [MA.get([128, 8, 128]) for _ in range(2)]
        cnat = [MA.get([128, 4, 64]) for _ in range(2)]
        mwork = MA.get([128, 2, 128])
        for nm, key in (("s5_a_re", "are"), ("s5_a_im", "aim")):
            v = W[nm][l].rearrange("(j e) p -> e p j", e=2)
            for e in range(2):
                P.dma("sp", sm[key][e * 64:(e + 1) * 64, :], v[e], "D_s5p", writes=[key])
        ld = W["s5_log_dt"][l]
        for e in range(2):
            src = AP(ld.tensor, ld.offset + e, [[0, 64], [2, 16]])
            P.dma("sp", sm["ldt"][e * 64:(e + 1) * 64, :], src, "D_s5p", writes=["ldt"])
        S = lambda nm: sm[nm][:, :]
        ACT("activation", ["ldt"], ["ldt"], out=S("ldt"), in_=S("ldt"), func=AF.Exp)
        DVE("tensor_scalar", ["are"], ["are"], out=S("are"), in0=S("are"), scalar1=-1e-4, scalar2=None, op0=ALU.min)
        DVE("tensor_tensor", ["ldt", "are"], ["a"], out=S("a"), in0=S("ldt"), in1=S("are"), op=ALU.mult)
        ACT("activation", ["a"], ["rho"], out=rho[:, :], in_=S("a"), func=AF.Exp)
        DVE("tensor_tensor", ["ldt", "aim"], ["th"], out=S("th"), in0=S("ldt"), in1=S("aim"), op=ALU.mult)
        for dst, sh in (("c1", 0.5 * PI), ("s1", 0.0)):
            DVE("tensor_scalar", ["th"], [dst], out=S(dst), in0=S("th"), scalar1=sh, scalar2=None, op0=ALU.add)
            for _ in range(6):
                DVE("tensor_scalar", [dst], ["c"], out=S("c"), in0=S(dst), scalar1=PI, scalar2=-2 * PI, op0=ALU.is_gt, op1=ALU.mult)
                DVE("tensor_tensor", [dst, "c"], [dst], out=S(dst), in0=S(dst), in1=S("c"), op=ALU.add)
            ACT("activation", [dst], [dst], out=S(dst), in_=S(dst), func=AF.Sin)
        DVE("tensor_tensor", ["rho", "c1"], ["a"], out=S("a"), in0=rho[:, :], in1=S("c1"), op=ALU.mult)
        DVE("tensor_scalar", ["a"], ["a"], out=S("a"), in0=S("a"), scalar1=-1.0, scalar2=None, op0=ALU.add)
        DVE("tensor_tensor", ["rho", "s1"], ["b"], out=S("b"), in0=rho[:, :], in1=S("s1"), op=ALU.mult)
        DVE("tensor_tensor", ["are"], ["den"], out=S("den"), in0=S("are"), in1=S("are"), op=ALU.mult)
        DVE("tensor_tensor", ["aim"], ["c"], out=S("c"), in0=S("aim"), in1=S("aim"), op=ALU.mult)
        DVE("tensor_tensor", ["den", "c"], ["den"], out=S("den"), in0=S("den"), in1=S("c"), op=ALU.add)
        DVE("reciprocal", ["den"], ["den"], out=S("den"), in_=S("den"))
        DVE("tensor_tensor", ["a", "are"], ["fre"], out=S("fre"), in0=S("a"), in1=S("are"), op=ALU.mult)
        DVE("tensor_tensor", ["b", "aim"], ["c"], out=S("c"), in0=S("b"), in1=S("aim"), op=ALU.mult)
        DVE("tensor_tensor", ["fre", "c"], ["fre"], out=S("fre"), in0=S("fre"), in1=S("c"), op=ALU.add)
        DVE("tensor_tensor", ["fre", "den"], ["fre"], out=S("fre"), in0=S("fre"), in1=S("den"), op=ALU.mult)
        DVE("tensor_tensor", ["b", "are"], ["fim"], out=S("fim"), in0=S("b"), in1=S("are"), op=ALU.mult)
        DVE("tensor_tensor", ["a", "aim"], ["c"], out=S("c"), in0=S("a"), in1=S("aim"), op=ALU.mult)
        DVE("tensor_tensor", ["fim", "c"], ["fim"], out=S("fim"), in0=S("fim"), in1=S("c"), op=ALU.subtract)
        DVE("tensor_tensor", ["fim", "den"], ["fim"], out=S("fim"), in0=S("fim"), in1=S("den"), op=ALU.mult)
        DVE("tensor_copy", ["c1"], ["cosT"], out=cosT[:, :, 0], in_=S("c1"))
        DVE("tensor_copy", ["s1"], ["sinT"], out=sinT[:, :, 0], in_=S("s1"))
        m = 1
        t1 = tt[0][:, :, :].rearrange("p a b -> p (a b)").rearrange("p (a b) -> p a b", a=16)
        t2 = tt[1][:, :, :].rearrange("p a b -> p (a b)").rearrange("p (a b) -> p a b", a=16)
        while m < 128:
            cm = bc(cosT[:, :, m - 1:m], [128, 16, m])
            smm = bc(sinT[:, :, m - 1:m], [128, 16, m])
            DVE("tensor_tensor", ["cosT"], ["t1"], out=t1[:, :, 0:m], in0=cosT[:, :, 0:m], in1=cm, op=ALU.mult)
            DVE("tensor_tensor", ["sinT"], ["t2"], out=t2[:, :, 0:m], in0=sinT[:, :, 0:m], in1=smm, op=ALU.mult)
            DVE("tensor_tensor", ["t1", "t2"], ["cosN"], out=cosT[:, :, m:2 * m], in0=t1[:, :, 0:m], in1=t2[:, :, 0:m], op=ALU.subtract)
            DVE("tensor_tensor", ["sinT"], ["t1"], out=t1[:, :, 0:m], in0=sinT[:, :, 0:m], in1=cm, op=ALU.mult)
            DVE("tensor_tensor", ["cosT"], ["t2"], out=t2[:, :, 0:m], in0=cosT[:, :, 0:m], in1=smm, op=ALU.mult)
            DVE("tensor_tensor", ["t1", "t2"], ["sinT"], out=sinT[:, :, m:2 * m], in0=t1[:, :, 0:m], in1=t2[:, :, 0:m], op=ALU.add)
            DVE("engine_nop", ["cosN"], ["cosT"])
            m *= 2
        for part, nm in enumerate(("s5_b_re", "s5_b_im")):
            POOL("memset", [], [f"bw{part}"], bw[part][:, :, :], 0.0)
            src0 = W[nm][l]
            for e in range(2):
                dv = bw[part][e * 64:(e + 1) * 64, :, :]
                for a4 in range(4):
                    dst = AP(dv.tensor, dv.offset + e * 16 + a4 * 512, [list(dv.ap[0]), [160, 4], [1, 16]])
                    src = AP(src0.tensor, src0.offset + e * 1024 + a4 * 8192, [[16, 64], [2048, 4], [1, 16]])
                    P.dma("sp", dst, src, "D_s5b", writes=[f"bw{part}"])
        for hf in range(2):
            js = slice(hf * 8, hf * 8 + 8)
            fre_b = bc(sm["fre"][:, js].unsqueeze(2), [128, 8, 128])
            fim_b = bc(sm["fim"][:, js].unsqueeze(2), [128, 8, 128])
            DVE("tensor_tensor", ["bw0", "fre"], ["t1"], out=tt[0][:, :, :], in0=bw[0][:, js, :], in1=fre_b, op=ALU.mult)
            DVE("tensor_tensor", ["bw1", "fim"], ["t2"], out=tt[1][:, :, :], in0=bw[1][:, js, :], in1=fim_b, op=ALU.mult)
            DVE("tensor_tensor", ["t1", "t2"], ["t1"], out=tt[0][:, :, :], in0=tt[0][:, :, :], in1=tt[1][:, :, :], op=ALU.subtract)
            DVE("tensor_tensor", ["bw0", "fim"], ["t2"], out=tt[1][:, :, :], in0=bw[0][:, js, :], in1=fim_b, op=ALU.mult)
            DVE("tensor_copy", ["t1"], ["bw0"], out=bw[0][:, js, :], in_=tt[0][:, :, :])
            DVE("tensor_tensor", ["bw1", "fre"], ["t1"], out=tt[0][:, :, :], in0=bw[1][:, js, :], in1=fre_b, op=ALU.mult)
            DVE("tensor_tensor", ["t1", "t2"], ["bw1"], out=bw[1][:, js, :], in0=tt[0][:, :, :], in1=tt[1][:, :, :], op=ALU.add)
        for part in range(2):
            for j4 in range(4):
                b = (part * 4 + j4) % 4
                for q in range(4):
                    j = j4 * 4 + q
                    PE("transpose", [f"bw{part}", "ident"], [f"ps{b}"], out=psb[b][:, q * 128:(q + 1) * 128],
                       in_=bw[part][:, j, :], identity=ident[:, :])
                ACT("copy", [f"ps{b}"], ["BT"], out=BT[:, part, j4 * 4:j4 * 4 + 4, :], in_=ps3(b, 128))
        for part, nm in enumerate(("s5_c_re", "s5_c_im")):
            P.dma("sp", cnat[part], W[nm][l].rearrange("(G g) i p -> (g i) G p", G=4), "D_s5c", writes=[f"cnat{part}"])
        it = 0
        for part in range(2):
            for j4 in range(4):
                b = 4 + (part * 4 + j4) % 4
                for q in range(4):
                    mw = it % 2
                    it += 1
                    for e in range(2):
                        DVE("tensor_scalar", [f"cnat{part}", "gmask"], [f"mw{mw}"], out=mwork[:, mw, e * 64:(e + 1) * 64],
                            in0=cnat[part][:, j4, :], scalar1=gmask[:, 2 * q + e:2 * q + e + 1], scalar2=None, op0=ALU.mult)
                    PE("transpose", [f"mw{mw}", "ident"], [f"ps{b}"], out=psb[b][:, q * 128:(q + 1) * 128],
                       in_=mwork[:, mw, :], identity=ident[:, :])
                ACT("mul", [f"ps{b}"], ["CT"], out=CT[:, part, j4 * 4:j4 * 4 + 4, :], in_=ps3(b, 128),
                    mul=(1.0 if part == 0 else -1.0))
        P.barrier()
        MA.off = mark
        uT = MA.get([128, 4, 128])
        uTb = MA.get([128, 4, 128], BF16)
        t1 = MA.get([128, 4, 128])
        t2 = MA.get([128, 4, 128])
        zre = MA.get([128, 4, 128])
        zim = MA.get([128, 4, 128])
        t3 = MA.get([128, 4, 128])
        t4 = MA.get([128, 4, 128])
        wre = MA.get([128, 4, 128])
        wim = MA.get([128, 4, 128])
        sreb = MA.get([128, 4, 128], BF16)
        simb = MA.get([128, 4, 128], BF16)
        yv = MA.get([128, 4, 128])
        yg = MA.get([128, 4, 128])
        tmp = MA.get([128, 4, 128])
        sq = MA.get([128, 4, 128])
        ygb = MA.get([128, 4, 128], BF16)
        ynT = MA.get([128, 4, 128], BF16)
        rs = MA.get([128, 128])
        s1 = MA.get([128, 4])
        s2 = MA.get([128, 4])
        DVE("memset", [], ["Sre"], Sre[:, :], 0.0)
        DVE("memset", [], ["Sim"], Sim[:, :], 0.0)

        def s5_front(i):
            n = tile_n(i)
            inproj_fm(i, wbig, 0, 4, [0])
            ACT("copy", ["ps0"], ["uT"], out=uT[:, :, 0:n], in_=ps3(0, n))
            ACT("copy", ["ps0"], ["uTb"], out=uTb[:, :, 0:n], in_=ps3(0, n))

        def s5_G(i, G):
            n = tile_n(i)
            bR = 1 + (G % 2) * 2
            bI = bR + 1
            for q in range(4):
                j = 4 * G + q
                PE("matmul", ["BT", "uTb"], [f"ps{bR}"], psb[bR][:, q * 128:q * 128 + n], lhsT=BT[:, 0, j, :],
                   rhs=uTb[:, G, 0:n], start=True, stop=True)
                PE("matmul", ["BT", "uTb"], [f"ps{bI}"], psb[bI][:, q * 128:q * 128 + n], lhsT=BT[:, 1, j, :],
                   rhs=uTb[:, G, 0:n], start=True, stop=True)
            cG = cosT[:, 4 * G:4 * G + 4, 0:n]
            sG = sinT[:, 4 * G:4 * G + 4, 0:n]
            pR, pI = ps3(bR, n), ps3(bI, n)
            T1, T2, T3, T4 = t1[:, :, 0:n], t2[:, :, 0:n], t3[:, :, 0:n], t4[:, :, 0:n]
            DVE("tensor_tensor", [f"ps{bR}"], ["t1"], out=T1, in0=cG, in1=pR, op=ALU.mult)
            DVE("tensor_tensor", [f"ps{bI}"], ["t2"], out=T2, in0=sG, in1=pI, op=ALU.mult)
            DVE("tensor_tensor", [f"ps{bI}"], ["t3"], out=T3, in0=cG, in1=pI, op=ALU.mult)
            DVE("tensor_tensor", [f"ps{bR}"], ["t4"], out=T4, in0=sG, in1=pR, op=ALU.mult)
            DVE("tensor_tensor", ["t1", "t2"], ["zre"], out=zre[:, :, 0:n], in0=T1, in1=T2, op=ALU.add)
            DVE("tensor_tensor", ["t3", "t4"], ["zim"], out=zim[:, :, 0:n], in0=T3, in1=T4, op=ALU.subtract)
            for q in range(4):
                j = 4 * G + q
                rb = bc(rho[:, j:j + 1], [128, n])
                DVE("tensor_tensor_scan", ["zre", "Sre"], [f"wre{q}"], out=wre[:, q, 0:n], data0=rb, data1=zre[:, q, 0:n],
                    initial=Sre[:, j:j + 1], op0=ALU.mult, op1=ALU.add)
            for q in range(4):
                j = 4 * G + q
                rb = bc(rho[:, j:j + 1], [128, n])
                DVE("tensor_tensor_scan", ["zim", "Sim"], [f"wim{q}"], out=wim[:, q, 0:n], data0=rb, data1=zim[:, q, 0:n],
                    initial=Sim[:, j:j + 1], op0=ALU.mult, op1=ALU.add)
            WRK = [f"wre{q}" for q in range(4)]
            WIK = [f"wim{q}" for q in range(4)]
            WR, WI = wre[:, :, 0:n], wim[:, :, 0:n]
            DVE("tensor_tensor", WRK, ["t1"], out=T1, in0=cG, in1=WR, op=ALU.mult)
            DVE("tensor_tensor", WIK, ["t2"], out=T2, in0=sG, in1=WI, op=ALU.mult)
            DVE("tensor_tensor", WRK, ["t3"], out=T3, in0=sG, in1=WR, op=ALU.mult)
            DVE("tensor_tensor", WIK, ["t4"], out=T4, in0=cG, in1=WI, op=ALU.mult)
            DVE("tensor_tensor", ["t1", "t2"], ["sreb"], out=sreb[:, :, 0:n], in0=T1, in1=T2, op=ALU.subtract)
            DVE("tensor_tensor", ["t3", "t4"], ["simb"], out=simb[:, :, 0:n], in0=T3, in1=T4, op=ALU.add)
            DVE("tensor_tensor", ["t1", "t2"], ["Sre"], out=Sre[:, 4 * G:4 * G + 4], in0=t1[:, :, n - 1], in1=t2[:, :, n - 1],
                op=ALU.subtract)
            DVE("tensor_tensor", ["t3", "t4"], ["Sim"], out=Sim[:, 4 * G:4 * G + 4], in0=t3[:, :, n - 1], in1=t4[:, :, n - 1],
                op=ALU.add)
            for q in range(4):
                j = 4 * G + q
                PE("matmul", ["CT", "sreb"], ["ps5"], psb[5][:, G * 128:G * 128 + n], lhsT=CT[:, 0, j, :],
                   rhs=sreb[:, q, 0:n], start=(q == 0), stop=False)
                PE("matmul", ["CT", "simb"], ["ps5"], psb[5][:, G * 128:G * 128 + n], lhsT=CT[:, 1, j, :],
                   rhs=simb[:, q, 0:n], start=False, stop=(q == 3))

        def s5_tail(i, seg):
            n = tile_n(i)
            YV, YG, TMP = yv[:, :, 0:n], yg[:, :, 0:n], tmp[:, :, 0:n]
            if seg == 0:
                for c in range(4):
                    DVE("scalar_tensor_tensor", ["uT", "pc", "ps5"], ["yv"], out=yv[:, c, 0:n], in0=uT[:, c, 0:n],
                        scalar=pc[:, c:c + 1], in1=psb[5][:, c * 128:c * 128 + n], op0=ALU.mult, op1=ALU.add)
                ACT("activation", ["yv"], ["tmp"], out=TMP, in_=YV, func=AF.Square)
            elif seg == 1:
                DVE("tensor_scalar", ["tmp"], ["tmp"], out=TMP, in0=TMP, scalar1=0.044715, scalar2=1.0, op0=ALU.mult, op1=ALU.add)
                DVE("tensor_tensor", ["tmp", "yv"], ["tmp"], out=TMP, in0=TMP, in1=YV, op=ALU.mult)
                ACT("activation", ["tmp"], ["tmp"], out=TMP, in_=TMP, func=AF.Sigmoid, scale=1.5957691216057308)
            elif seg == 2:
                DVE("tensor_tensor", ["tmp", "yv"], ["yg"], out=YG, in0=TMP, in1=YV, op=ALU.mult)
                ACT("copy", ["yg"], ["ygb"], out=ygb[:, :, 0:n], in_=YG)
                for oc in range(4):
                    for kc in range(4):
                        PE("matmul", ["wglu", "ygb"], ["ps6"], psb[6][:, oc * 128:oc * 128 + n],
                           lhsT=wglu[:, kc, oc * 128:(oc + 1) * 128], rhs=ygb[:, kc, 0:n], start=(kc == 0), stop=(kc == 3))
                ACT("activation", ["ps6"], ["tmp"], out=TMP, in_=ps3(6, n), func=AF.Sigmoid)
            elif seg == 3:
                DVE("tensor_tensor", ["yg", "tmp"], ["yg"], out=YG, in0=YG, in1=TMP, op=ALU.mult)
                ACT("activation", ["yg"], ["sq"], out=sq[:, :, 0:n], in_=YG, func=AF.Square)
                for c in range(4):
                    PE("matmul", ["sq", "ones"], ["ps7"], psb[7][:, 0:n], lhsT=ones[:, :], rhs=sq[:, c, 0:n],
                       start=(c == 0), stop=(c == 3))
            elif seg == 4:
                DVE("tensor_scalar", ["ps7"], ["rs"], out=rs[:, 0:n], in0=psb[7][:, 0:n], scalar1=1.0 / 512, scalar2=EPS,
                    op0=ALU.mult, op1=ALU.add)
                ACT("activation", ["rs"], ["rs"], out=rs[:, 0:n], in_=rs[:, 0:n], func=AF.Ln)
                ACT("activation", ["rs"], ["rs"], out=rs[:, 0:n], in_=rs[:, 0:n], func=AF.Exp, scale=-0.5)
            elif seg == 5:
                for c in range(4):
                    DVE("scalar_tensor_tensor", ["yg", "pc", "rs"], ["ynT"], out=ynT[:, c, 0:n], in0=yg[:, c, 0:n],
                        scalar=pc[:, 4 + c:5 + c], in1=rs[:, 0:n], op0=ALU.mult, op1=ALU.mult)
                outproj(i, ynT, wo)

        for i in range(NT + 1):
            if i > 0:
                s5_tail(i - 1, 0)
            if i < NT:
                s5_front(i)
            for G in range(4):
                if i < NT:
                    s5_G(i, G)
                if i > 0:
                    s5_tail(i - 1, G + 1)
            if i > 0:
                s5_tail(i - 1, 5)

    def v3(ap, a):
        return ap.rearrange("p (a b) -> p a b", a=a)

    def load_rb(rb, items, l):
        for off, nm, ln in items:
            v = W[nm][l]
            src = AP(v.tensor, v.offset, [[0, 128], [1, ln]])
            P.dma("sp", rb[:, off:off + ln], src, "D_rb", writes=["rb"])

    def ssd_phase(l):
        P.barrier()
        WA.reset()
        MA.reset()
        wbig = WA.get([128, 8, 1544], BF16)
        wo = WA.get([128, 4, D], BF16)
        load_win(wbig, l, 3080, 4624)
        load_wout(wo, l, 1024)
        load_cols([W["ssd_conv_w"][l].rearrange("k (c p) -> (k c) p", p=128),
                   W["ssd_conv_b"][l].rearrange("(c p) -> c p", p=128)])
        rb = MA.get([128, 536])
        load_rb(rb, ((0, "ssd_a_log", 8), (8, "ssd_dt_bias", 8), (16, "ssd_d", 8), (24, "ssd_norm", 512)), l)
        ACT("activation", ["rb"], ["rb"], out=rb[:, 0:8], in_=rb[:, 0:8], func=AF.Exp)
        DVE("tensor_scalar", ["rb"], ["rb"], out=rb[:, 0:8], in0=rb[:, 0:8], scalar1=-1.0, scalar2=None, op0=ALU.mult)
        cb = MA.get([128, 8, 131])
        xbc = MA.get([128, 8, 128])
        ctmp = MA.get([128, 4, 128])
        bcT2 = [MA.get([128, 4, 128], BF16) for _ in range(2)]
        xs_tm2 = [MA.get([128, 512]) for _ in range(2)]
        B_tm2 = [MA.get([128, 256], BF16) for _ in range(2)]
        zs2 = [MA.get([128, 512]) for _ in range(2)]
        smf = [MA.get([128, 16]) for _ in range(2)]
        sm = MA.get([128, 48])
        acum, eac, dst_, dtd, cd, ss2 = (sm[:, 8 * q:8 * q + 8] for q in range(6))
        adtb = MA.get([128, 8, 128])
        DT = MA.get([128, 8, 128])
        ST = MA.get([128, 8, 128], BF16)
        xdt = MA.get([128, 512], BF16)
        xdtd = MA.get([128, 512], BF16)
        Sst = MA.get([128, 512])
        Sb = MA.get([128, 2, 256], BF16)
        y = MA.get([128, 512])
        y1 = MA.get([128, 512])
        ynT = MA.get([128, 4, 128], BF16)
        DVE("memset", [], ["cbS"], cb[:, :, 0:3], 0.0)
        DVE("memset", [], ["Sst"], Sst[:, :], 0.0)
        DVE("memset", [], ["Sb"], Sb[:, :, :], 0.0)

        def ssd_front(i):
            n = tile_n(i)
            p = i % 2
            dtv, adt = smf[p][:, 0:8], smf[p][:, 8:16]
            bcT, xs_tm, B_tm, zs = bcT2[p], xs_tm2[p], B_tm2[p], zs2[p]
            inproj_tm(i, wbig, 0, 512, 0)
            inproj_tm(i, wbig, 1536, 8, 1)
            DVE("tensor_tensor", ["ps1", "rb"], [f"dtv{p}"], out=dtv[:n, :], in0=psb[1][:n, 0:8], in1=rb[:n, 8:16], op=ALU.add)
            ACT("activation", [f"dtv{p}"], [f"dtv{p}"], out=dtv[:n, :], in_=dtv[:n, :], func=AF.Exp)
            ACT("activation", [f"dtv{p}"], [f"dtv{p}"], out=dtv[:n, :], in_=dtv[:n, :], func=AF.Ln, bias=1.0)
            DVE("tensor_tensor", [f"dtv{p}", "rb"], [f"adt{p}"], out=adt[:n, :], in0=dtv[:n, :], in1=rb[:n, 0:8], op=ALU.mult)
            inproj_fm(i, wbig, 512, 8, [2, 3])
            ACT("copy", ["ps2"], ["cbS"], out=cb[:, 0:4, 3:3 + n], in_=ps3(2, n))
            ACT("copy", ["ps3"], ["cbS"], out=cb[:, 4:8, 3:3 + n], in_=ps3(3, n))
            conv_fm(cb, "cbS", xbc, "xbc", 8, n, 0, 32, ctmp, "ctmp")
            ACT("activation", ["xbc"], ["xbc"], out=xbc[:, :, 0:n], in_=xbc[:, :, 0:n], func=AF.Silu)
            ACT("activation", ["ps0"], [f"zs{p}"], out=zs[:n, :], in_=psb[0][:n, 0:512], func=AF.Silu)
            ACT("copy", ["xbc"], [f"bcT{p}"], out=bcT[:, :, 0:n], in_=xbc[:, 4:8, 0:n])
            for c in range(4):
                PE("transpose", ["xbc", "ident"], ["ps2"], out=psb[2][:n, c * 128:(c + 1) * 128], in_=xbc[:, c, 0:n],
                   identity=ident[:, :])
            for c in range(2):
                PE("transpose", ["xbc", "ident"], ["ps3"], out=psb[3][:n, c * 128:(c + 1) * 128], in_=xbc[:, 4 + c, 0:n],
                   identity=ident[:, :])
            DVE("tensor_copy", ["ps2"], [f"xs_tm{p}"], out=xs_tm[:n, :], in_=psb[2][:n, 0:512])
            ACT("copy", ["ps3"], [f"B_tm{p}"], out=B_tm[:n, :], in_=psb[3][:n, 0:256])

        def ssd_back(i):
            n = tile_n(i)
            p = i % 2
            dtv, adt = smf[p][:, 0:8], smf[p][:, 8:16]
            bcT, xs_tm, B_tm, zs = bcT2[p], xs_tm2[p], B_tm2[p], zs2[p]
            kb, kx, kB, kz, kd, ka = f"bcT{p}", f"xs_tm{p}", f"B_tm{p}", f"zs{p}", f"dtv{p}", f"adt{p}"
            PE("matmul", [ka, "triu"], ["ps5"], psb[5][:n, 0:8], lhsT=triu[:n, :n], rhs=adt[:n, :], start=True, stop=True)
            DVE("tensor_copy", ["ps5"], ["acum"], out=acum[:n, :], in_=psb[5][:n, 0:8])
            ACT("activation", ["ps5"], ["eac"], out=eac[:n, :], in_=psb[5][:n, 0:8], func=AF.Exp)
            DVE("tensor_copy", [ka], ["adtb"], out=adtb[:n, :, :], in_=bc(adt[:n, :].unsqueeze(2), [n, 8, 128]))
            for hh in range(8):
                b = 6 + hh // 4
                PE("matmul", ["adtb", "triu"], [f"ps{b}"], psb[b][:, (hh % 4) * 128:(hh % 4) * 128 + n],
                   lhsT=adtb[:n, hh, :], rhs=triu[:n, :n], start=True, stop=True)
            for hh in range(8):
                b = 6 + hh // 4
                DVE("scalar_tensor_tensor", [f"ps{b}", "acum", "mincT"], ["DT"], out=DT[:n, hh, 0:n],
                    in0=psb[b][:n, (hh % 4) * 128:(hh % 4) * 128 + n], scalar=acum[:n, hh:hh + 1], in1=mincT[:n, :n],
                    op0=ALU.subtract, op1=ALU.add)
            ACT("activation", ["DT"], ["DT"], out=DT[:n, :, 0:n], in_=DT[:n, :, 0:n], func=AF.Exp)
            for hb in range(2):
                pv = v3(psb[6 + hb][:, :], 4)
                ACT("activation", [f"ps{6 + hb}"], ["cd"], out=cd[:, 4 * hb:4 * hb + 4], in_=pv[:, :, n - 1], func=AF.Exp)
                DVE("tensor_tensor", [f"ps{6 + hb}", "acum"], ["dst"], out=dst_[:n, 4 * hb:4 * hb + 4], in0=pv[:n, :, n - 1],
                    in1=acum[:n, 4 * hb:4 * hb + 4], op=ALU.subtract)
            ACT("activation", ["dst"], ["dst"], out=dst_[:n, :], in_=dst_[:n, :], func=AF.Exp)
            DVE("tensor_tensor", [kd, "dst"], ["dtd"], out=dtd[:n, :], in0=dtv[:n, :], in1=dst_[:n, :], op=ALU.mult)
            for g in range(2):
                PE("matmul", [kb], ["ps4"], psb[4][:n, g * 128:g * 128 + n], lhsT=bcT[:, g, 0:n], rhs=bcT[:, 2 + g, 0:n],
                   start=True, stop=True)
            for g in range(2):
                DVE("tensor_tensor", ["DT", "ps4"], ["ST"], out=ST[:n, 4 * g:4 * g + 4, 0:n], in0=DT[:n, 4 * g:4 * g + 4, 0:n],
                    in1=bc(psb[4][:n, g * 128:g * 128 + n].unsqueeze(1), [n, 4, n]), op=ALU.mult)
            DVE("tensor_tensor", [kx, kd], ["xdt"], out=v3(xdt[:n, :], 8), in0=v3(xs_tm[:n, :], 8),
                in1=bc(dtv[:n, :].unsqueeze(2), [n, 8, 64]), op=ALU.mult)
            DVE("tensor_tensor", [kx, "dtd"], ["xdtd"], out=v3(xdtd[:n, :], 8), in0=v3(xs_tm[:n, :], 8),
                in1=bc(dtd[:n, :].unsqueeze(2), [n, 8, 64]), op=ALU.mult)
            for hh in range(8):
                PE("matmul", ["ST", "xdt"], ["ps5"], psb[5][:n, hh * 64:(hh + 1) * 64], lhsT=ST[:n, hh, 0:n],
                   rhs=xdt[:n, hh * 64:(hh + 1) * 64], start=True, stop=True)
            for g in range(2):
                PE("matmul", [kb, "Sb"], ["ps4"], psb[4][:n, g * 256:(g + 1) * 256], lhsT=bcT[:, 2 + g, 0:n],
                   rhs=Sb[:, g, :], start=True, stop=True)
            for g in range(2):
                PE("matmul", [kB, "xdtd"], ["ps6"], psb[6][:, g * 256:(g + 1) * 256], lhsT=B_tm[:n, g * 128:(g + 1) * 128],
                   rhs=xdtd[:n, g * 256:(g + 1) * 256], start=True, stop=True)
            DVE("tensor_tensor", ["Sst", "cd"], ["Sst"], out=v3(Sst[:, :], 8), in0=v3(Sst[:, :], 8),
                in1=bc(cd[:, :].unsqueeze(2), [128, 8, 64]), op=ALU.mult)
            DVE("tensor_tensor", ["Sst", "ps6"], ["Sst"], out=Sst[:, :], in0=Sst[:, :], in1=psb[6][:, 0:512], op=ALU.add)
            ACT("copy", ["Sst"], ["Sb"], out=Sb[:, :, :], in_=v3(Sst[:, :], 2))
            DVE("tensor_tensor", ["ps4", "eac"], ["y1"], out=v3(y1[:n, :], 8), in0=v3(psb[4][:n, 0:512], 8),
                in1=bc(eac[:n, :].unsqueeze(2), [n, 8, 64]), op=ALU.mult)
            DVE("tensor_tensor", ["y1", "ps5"], ["y"], out=y[:n, :], in0=y1[:n, :], in1=psb[5][:n, 0:512], op=ALU.add)
            DVE("tensor_tensor", [kx, "rb"], ["y1"], out=v3(y1[:n, :], 8), in0=v3(xs_tm[:n, :], 8),
                in1=bc(rb[:n, 16:24].unsqueeze(2), [n, 8, 64]), op=ALU.mult)
            DVE("tensor_tensor", ["y", "y1"], ["y"], out=y[:n, :], in0=y[:n, :], in1=y1[:n, :], op=ALU.add)
            DVE("tensor_tensor", ["y", kz], ["y"], out=y[:n, :], in0=y[:n, :], in1=zs[:n, :], op=ALU.mult)
            for g in range(2):
                ACT("activation", ["y"], ["junk", "ss2"], out=junk[:n, 0:256], in_=y[:n, g * 256:(g + 1) * 256], func=AF.Square,
                    accum_out=ss2[:n, g:g + 1])
            rsqrt_inplace(ss2[:n, 0:2], "ss2", 1.0 / 256, EPS)
            for g in range(2):
                DVE("scalar_tensor_tensor", ["y", "ss2", "rb"], ["y"], out=y[:n, g * 256:(g + 1) * 256],
                    in0=y[:n, g * 256:(g + 1) * 256], scalar=ss2[:n, g:g + 1], in1=rb[:n, 24 + g * 256:24 + (g + 1) * 256],
                    op0=ALU.mult, op1=ALU.mult)
            tm_to_ynT(i, y, "y", ynT)
            outproj(i, ynT, wo)

        ssd_front(0)
        for i in range(NT):
            P.capture = []
            ssd_back(i)
            A = P.capture
            P.capture = []
            if i + 1 < NT:
                ssd_front(i + 1)
            B = P.capture
            P.capture = None
            P.replay_merged(A, B)

    def gdn_phase(l):
        P.barrier()
        WA.reset()
        MA.reset()
        wbig = WA.get([128, 8, 2056], BF16)
        wo = WA.get([128, 4, D], BF16)
        load_win(wbig, l, 1024, 3080)
        load_wout(wo, l, 512)
        load_cols([W["gdn_conv_w"][l].rearrange("k (c p) -> (k c) p", p=128)])
        rb = MA.get([128, 136])
        load_rb(rb, ((0, "gdn_a_log", 4), (4, "gdn_dt_bias", 4), (8, "gdn_norm", 128)), l)
        ACT("activation", ["rb"], ["rb"], out=rb[:, 0:4], in_=rb[:, 0:4], func=AF.Exp)
        DVE("tensor_scalar", ["rb"], ["rb"], out=rb[:, 0:4], in0=rb[:, 0:4], scalar1=-1.0, scalar2=None, op0=ALU.mult)
        cb = MA.get([128, 12, 131])
        qkv = MA.get([128, 12, 128])
        sq = MA.get([128, 8, 128])
        qT = MA.get([128, 4, 128], BF16)
        kT = MA.get([128, 4, 128], BF16)
        qdT = MA.get([128, 4, 128], BF16)
        zs = MA.get([128, 512])
        sm = MA.get([128, 40])
        bg, gcum, egl, kds, egc, bge, ss4 = (sm[:, 0:8], sm[:, 8:12], sm[:, 12:16], sm[:, 16:20], sm[:, 20:24],
                                             sm[:, 24:28], sm[:, 28:32])
        gb = MA.get([128, 4, 128])
        DTi = MA.get([128, 4, 128])
        Dst = MA.get([128, 4, 128])
        erow = MA.get([128, 4, 128])
        vb = MA.get([128, 4, 128])
        kbg = MA.get([128, 4, 128])
        kdec = MA.get([128, 4, 128], BF16)
        XS = Arena(xs[:, :, :].rearrange("p a b -> p (a b)"), 2048)
        Xb = [XS.get([128, 128]) for _ in range(4)]
        XTb = [XS.get([128, 128]) for _ in range(4)]
        PTh = [XS.get([128, 128]) for _ in range(4)]
        u_sh = [XS.get([128, 128]) for _ in range(4)]
        attnTh = [MA.get([128, 128], BF16) for _ in range(4)]
        wTh = [MA.get([128, 128], BF16) for _ in range(4)]
        vnewh = [MA.get([128, 128], BF16) for _ in range(4)]
        Sf = MA.get([128, 4, 128])
        Sb = MA.get([128, 4, 128], BF16)
        o = MA.get([128, 512])
        ynT = MA.get([128, 4, 128], BF16)
        DVE("memset", [], ["cbG"], cb[:, :, 0:3], 0.0)
        DVE("memset", [], ["Sf"], Sf[:, :, :], 0.0)
        DVE("memset", [], ["Sb"], Sb[:, :, :], 0.0)
        for i in range(NT):
            n = tile_n(i)
            nlev = int(round(math.log2(n))) - 1
            inproj_fm(i, wbig, 0, 12, [0, 1, 2])
            for b in range(3):
                ACT("copy", [f"ps{b}"], ["cbG"], out=cb[:, 4 * b:4 * b + 4, 3:3 + n], in_=ps3(b, n))
            inproj_tm(i, wbig, 1536, 512, 7)
            conv_fm(cb, "cbG", qkv, "qkv", 12, n, 0, None, sq, "sq")
            ACT("activation", ["qkv"], ["qkv"], out=qkv[:, :, 0:n], in_=qkv[:, :, 0:n], func=AF.Silu)
            ACT("activation", ["ps7"], ["zs"], out=zs[:n, :], in_=psb[7][:n, 0:512], func=AF.Silu)
            ACT("activation", ["qkv"], ["sq"], out=sq[:, :, 0:n], in_=qkv[:, 0:8, 0:n], func=AF.Square)
            for c in range(8):
                b = 3 + c // 4
                PE("matmul", ["sq", "ones"], [f"ps{b}"], psb[b][:, (c % 4) * 128:(c % 4) * 128 + n], lhsT=ones[:, :],
                   rhs=sq[:, c, 0:n], start=True, stop=True)
            for b in range(2):
                DVE("tensor_scalar", [f"ps{3 + b}"], ["sq"], out=sq[:, 4 * b:4 * b + 4, 0:n], in0=ps3(3 + b, n), scalar1=EPS,
                    scalar2=None, op0=ALU.add)
            ACT("activation", ["sq"], ["sq"], out=sq[:, :, 0:n], in_=sq[:, :, 0:n], func=AF.Ln)
            ACT("activation", ["sq"], ["sq"], out=sq[:, :, 0:n], in_=sq[:, :, 0:n], func=AF.Exp, scale=-0.5)
            DVE("tensor_tensor", ["qkv", "sq"], ["qkv"], out=qkv[:, 0:8, 0:n], in0=qkv[:, 0:8, 0:n], in1=sq[:, :, 0:n], op=ALU.mult)
            ACT("mul", ["qkv"], ["qT"], out=qT[:, :, 0:n], in_=qkv[:, 0:4, 0:n], mul=128.0 ** -0.5)
            ACT("copy", ["qkv"], ["kT"], out=kT[:, :, 0:n], in_=qkv[:, 4:8, 0:n])
            for c in range(4):
                PE("transpose", ["qkv", "ident"], ["ps5"], out=psb[5][:n, c * 128:(c + 1) * 128], in_=qkv[:, 4 + c, 0:n],
                   identity=ident[:, :])
            for c in range(4):
                PE("transpose", ["qkv", "ident"], ["ps6"], out=psb[6][:n, c * 128:(c + 1) * 128], in_=qkv[:, 8 + c, 0:n],
                   identity=ident[:, :])
            inproj_tm(i, wbig, 2048, 8, 0)
            ACT("activation", ["ps0"], ["bg"], out=bg[:n, 0:4], in_=psb[0][:n, 0:4], func=AF.Exp, scale=-1.0)
            DVE("tensor_scalar", ["bg"], ["bg"], out=bg[:n, 0:4], in0=bg[:n, 0:4], scalar1=1.0, scalar2=None, op0=ALU.add)
            DVE("reciprocal", ["bg"], ["bg"], out=bg[:n, 0:4], in_=bg[:n, 0:4])
            DVE("tensor_tensor", ["ps0", "rb"], ["bg"], out=bg[:n, 4:8], in0=psb[0][:n, 4:8], in1=rb[:n, 4:8], op=ALU.add)
            ACT("activation", ["bg"], ["bg"], out=bg[:n, 4:8], in_=bg[:n, 4:8], func=AF.Exp)
            ACT("activation", ["bg"], ["bg"], out=bg[:n, 4:8], in_=bg[:n, 4:8], func=AF.Ln, bias=1.0)
            DVE("tensor_tensor", ["bg", "rb"], ["bg"], out=bg[:n, 4:8], in0=bg[:n, 4:8], in1=rb[:n, 0:4], op=ALU.mult)
            PE("matmul", ["bg", "triu"], ["ps0"], psb[0][:n, 8:12], lhsT=triu[:n, :n], rhs=bg[:n, 4:8], start=True, stop=True)
            DVE("tensor_copy", ["ps0"], ["gcum"], out=gcum[:n, :], in_=psb[0][:n, 8:12])
            DVE("tensor_copy", ["bg"], ["gb"], out=gb[:n, :, :], in_=bc(bg[:n, 4:8].unsqueeze(2), [n, 4, 128]))
            for hh in range(4):
                PE("matmul", ["gb", "triu"], ["ps1"], psb[1][:, hh * 128:hh * 128 + n], lhsT=gb[:n, hh, :], rhs=triu[:n, :n],
                   start=True, stop=True)
            for hh in range(4):
                DVE("scalar_tensor_tensor", ["ps1", "gcum", "mincT"], ["DTi"], out=DTi[:n, hh, 0:n],
                    in0=psb[1][:n, hh * 128:hh * 128 + n], scalar=gcum[:n, hh:hh + 1], in1=mincT[:n, :n],
                    op0=ALU.subtract, op1=ALU.add)
                DVE("scalar_tensor_tensor", ["ps1", "gcum", "pstr"], ["Dst"], out=Dst[:n, hh, 0:n],
                    in0=psb[1][:n, hh * 128:hh * 128 + n], scalar=gcum[:n, hh:hh + 1], in1=pstr[:n, :n],
                    op0=ALU.subtract, op1=ALU.add)
            ACT("activation", ["DTi"], ["DTi"], out=DTi[:n, :, 0:n], in_=DTi[:n, :, 0:n], func=AF.Exp)
            ACT("activation", ["Dst"], ["Dst"], out=Dst[:n, :, 0:n], in_=Dst[:n, :, 0:n], func=AF.Exp, scale=-1.0)
            pv = v3(psb[1][:, :], 4)
            ACT("activation", ["ps1"], ["egl"], out=egl[:, :], in_=pv[:, :, n - 1], func=AF.Exp)
            DVE("tensor_tensor", ["ps1", "gcum"], ["kds"], out=kds[:n, :], in0=pv[:n, :, n - 1], in1=gcum[:n, :], op=ALU.subtract)
            ACT("activation", ["kds"], ["kds"], out=kds[:n, :], in_=kds[:n, :], func=AF.Exp)
            ACT("activation", ["gcum"], ["egc"], out=egc[:n, :], in_=gcum[:n, :], func=AF.Exp)
            DVE("tensor_tensor", ["bg", "egc"], ["bge"], out=bge[:n, :], in0=bg[:n, 0:4], in1=egc[:n, :], op=ALU.mult)
            ACT("activation", ["ps1"], ["erow"], out=erow[:, :, 0:n], in_=ps3(1, n), func=AF.Exp)
            DVE("tensor_tensor", ["qT", "erow"], ["qdT"], out=qdT[:, :, 0:n], in0=qT[:, :, 0:n], in1=erow[:, :, 0:n], op=ALU.mult)
            DVE("tensor_tensor", ["ps6", "bg"], ["vb"], out=vb[:n, :, :], in0=v3(psb[6][:n, :], 4),
                in1=bc(bg[:n, 0:4].unsqueeze(2), [n, 4, 128]), op=ALU.mult)
            DVE("tensor_tensor", ["ps5", "bge"], ["kbg"], out=kbg[:n, :, :], in0=v3(psb[5][:n, :], 4),
                in1=bc(bge[:n, :].unsqueeze(2), [n, 4, 128]), op=ALU.mult)
            DVE("tensor_tensor", ["ps5", "kds"], ["kdec"], out=kdec[:n, :, :], in0=v3(psb[5][:n, :], 4),
                in1=bc(kds[:n, :].unsqueeze(2), [n, 4, 128]), op=ALU.mult)
            for hh in range(4):
                pk = f"ps{hh}"
                PE("matmul", ["qkv"], [pk], psb[hh][:n, 0:n], lhsT=qkv[:, 4 + hh, 0:n], rhs=qkv[:, 4 + hh, 0:n],
                   start=True, stop=True)
                PE("matmul", ["kT", "qT"], [pk], psb[hh][:n, 128:128 + n], lhsT=kT[:, hh, 0:n], rhs=qT[:, hh, 0:n],
                   start=True, stop=True)
            for hh in range(4):
                pk = f"ps{hh}"
                DVE("scalar_tensor_tensor", [pk, "bg", "Dst"], [f"X{hh}"], out=Xb[hh][:n, :n], in0=psb[hh][:n, 0:n],
                    scalar=bg[:n, hh:hh + 1], in1=Dst[:n, hh, 0:n], op0=ALU.mult, op1=ALU.mult)
                DVE("tensor_tensor", [pk, "DTi"], [f"attnT{hh}"], out=attnTh[hh][:n, :n], in0=psb[hh][:n, 128:128 + n],
                    in1=DTi[:n, hh, 0:n], op=ALU.mult)
            for hh in range(4):
                PE("matmul", [f"X{hh}", "ident"], [f"ps{hh}"], psb[hh][:n, 256:256 + n], lhsT=Xb[hh][:n, :n], rhs=ident[:n, :n],
                   start=True, stop=True)
            for hh in range(4):
                pk = f"ps{hh}"
                ACT("copy", [pk], [f"XT{hh}"], out=XTb[hh][:n, :n], in_=psb[hh][:n, 256:256 + n])
                DVE("tensor_tensor", ["ident", pk], [f"PT{hh}"], out=PTh[hh][:n, :n], in0=ident[:n, :n],
                    in1=psb[hh][:n, 256:256 + n], op=ALU.subtract)
            for lev in range(nlev):
                last = (lev == nlev - 1)
                for hh in range(4):
                    pk = f"ps{hh}"
                    PE("matmul", [f"X{hh}", f"XT{hh}"], [pk], psb[hh][:n, 0:n], lhsT=XTb[hh][:n, :n], rhs=Xb[hh][:n, :n],
                       start=True, stop=True)
                for hh in range(4):
                    ACT("copy", [f"ps{hh}"], [f"X{hh}"], out=Xb[hh][:n, :n], in_=psb[hh][:n, 0:n])
                if not last:
                    for hh in range(4):
                        PE("transpose", [f"X{hh}", "ident"], [f"ps{hh}"], out=psb[hh][:n, 128:128 + n], in_=Xb[hh][:n, :n],
                           identity=ident[:n, :n])
                    for hh in range(4):
                        DVE("tensor_copy", [f"ps{hh}"], [f"XT{hh}"], out=XTb[hh][:n, :n], in_=psb[hh][:n, 128:128 + n])
                for hh in range(4):
                    b = 4 + hh % 2
                    PE("matmul", [f"X{hh}", f"PT{hh}"], [f"ps{b}"], psb[b][:n, (hh // 2) * 128:(hh // 2) * 128 + n],
                       lhsT=Xb[hh][:n, :n], rhs=PTh[hh][:n, :n], start=True, stop=True)
                for hh in range(4):
                    b = 4 + hh % 2
                    DVE("tensor_tensor", [f"ps{b}", f"PT{hh}"], [f"PT{hh}"], out=PTh[hh][:n, :n],
                        in0=psb[b][:n, (hh // 2) * 128:(hh // 2) * 128 + n], in1=PTh[hh][:n, :n], op=ALU.add)
            for hh in range(4):
                pk = f"ps{hh}"
                TT, TTk = PTh[hh], f"PT{hh}"
                PE("matmul", [TTk, "vb"], [pk], psb[hh][:n, 0:128], lhsT=TT[:n, :n], rhs=vb[:n, hh, :], start=True, stop=True)
                PE("matmul", [TTk, "kbg"], [pk], psb[hh][:, 128:128 + n], lhsT=kbg[:n, hh, :], rhs=TT[:n, :n],
                   start=True, stop=True)
            for hh in range(4):
                pk = f"ps{hh}"
                ACT("copy", [pk], [f"wT{hh}"], out=wTh[hh][:, 0:n], in_=psb[hh][:, 128:128 + n])
                DVE("tensor_copy", [pk], [f"u_s{hh}"], out=u_sh[hh][:n, :], in_=psb[hh][:n, 0:128])
            for hh in range(4):
                PE("matmul", [f"wT{hh}", "Sb"], [f"ps{hh}"], psb[hh][:n, 256:384], lhsT=wTh[hh][:, 0:n], rhs=Sb[:, hh, :],
                   start=True, stop=True)
            for hh in range(4):
                DVE("tensor_tensor", [f"u_s{hh}", f"ps{hh}"], [f"vnew{hh}"], out=vnewh[hh][:n, :], in0=u_sh[hh][:n, :],
                    in1=psb[hh][:n, 256:384], op=ALU.subtract)
            for hh in range(4):
                PE("matmul", ["qdT", "Sb"], ["ps7"], psb[7][:n, hh * 128:(hh + 1) * 128], lhsT=qdT[:, hh, 0:n], rhs=Sb[:, hh, :],
                   start=True, stop=False)
                PE("matmul", [f"attnT{hh}", f"vnew{hh}"], ["ps7"], psb[7][:n, hh * 128:(hh + 1) * 128], lhsT=attnTh[hh][:n, :n],
                   rhs=vnewh[hh][:n, :], start=False, stop=True)
                PE("matmul", ["kdec", f"vnew{hh}"], [f"ps{hh}"], psb[hh][:, 384:512], lhsT=kdec[:n, hh, :], rhs=vnewh[hh][:n, :],
                   start=True, stop=True)
            for hh in range(4):
                DVE("scalar_tensor_tensor", ["Sf", "egl", f"ps{hh}"], ["Sf"], out=Sf[:, hh, :], in0=Sf[:, hh, :],
                    scalar=egl[:, hh:hh + 1], in1=psb[hh][:, 384:512], op0=ALU.mult, op1=ALU.add)
                ACT("copy", ["Sf"], ["Sb"], out=Sb[:, hh, :], in_=Sf[:, hh, :])
            for hh in range(4):
                ACT("activation", ["ps7"], ["junk", "ss4"], out=junk[:n, 0:128], in_=psb[7][:n, hh * 128:(hh + 1) * 128],
                    func=AF.Square, accum_out=ss4[:n, hh:hh + 1])
            rsqrt_inplace(ss4[:n, 0:4], "ss4", 1.0 / 128, EPS)
            for hh in range(4):
                DVE("scalar_tensor_tensor", ["ps7", "ss4", "rb"], ["o"], out=o[:n, hh * 128:(hh + 1) * 128],
                    in0=psb[7][:n, hh * 128:(hh + 1) * 128], scalar=ss4[:n, hh:hh + 1], in1=rb[:n, 8:136],
                    op0=ALU.mult, op1=ALU.mult)
            DVE("tensor_tensor", ["o", "zs"], ["o"], out=o[:n, :], in0=o[:n, :], in1=zs[:n, :], op=ALU.mult)
            tm_to_ynT(i, o, "o", ynT)
            outproj(i, ynT, wo)

    order = ["ffn1", "lru", "s5", "ssd", "gdn", "ffn2"]
    done = False
    for l in range(nlayers):
        if l > 0:
            P.new_epoch()
        for ph in order:
            if done:
                break
            if ph == "ffn1":
                ffn_alloc(W["ffn1_w_gate"][l], W["ffn1_w_up"][l], W["ffn1_w_down"][l])
                norm_phase(W["ffn1_norm"][l])
                ffn_phase(W["ffn1_w_gate"][l], W["ffn1_w_up"][l], W["ffn1_w_down"][l])
                P.barrier()
                lru_load(l)
                norm_phase(W["mix_norm"][l])
            elif ph == "lru":
                lru_phase(l)
            elif ph == "s5":
                s5_phase(l)
            elif ph == "ssd":
                ssd_phase(l)
            elif ph == "gdn":
                gdn_phase(l)
            elif ph == "ffn2":
                P.barrier()
                ffn_alloc(W["ffn2_w_gate"][l], W["ffn2_w_up"][l], W["ffn2_w_down"][l])
                norm_phase(W["ffn2_norm"][l])
                ffn_phase(W["ffn2_w_gate"][l], W["ffn2_w_up"][l], W["ffn2_w_down"][l])
            if stop_after == (l, ph):
                done = True

    P.barrier()
    if dbg == "h":
        P.dma("sp", DBGd[0:16, :], h[0:16, 0, :], "D_dbg", reads=["h:0"])
        dv = DBGd[16:T, :].rearrange("(i p) d -> p i d", p=128)
        for i in range(1, NT):
            P.dma("sp", dv[:, i - 1, :], h[:, i, :], "D_dbg", reads=[f"h:{i}"])
        P.wait_sem("sp", "D_dbg")

    ov = OUTd.rearrange("(i p) d -> p i d", p=128)
    if final:
        WA.reset()
        grow = WA.get([128, D])
        fv = W["final_norm"]
        P.dma("sp", grow, AP(fv.tensor, fv.offset, [[0, 128], [1, D]]), "D_grow", writes=["grow"])
        for i in range(1, NT):
            ACT("activation", [f"h:{i}"], ["junk", "ss"], out=junk[:, :], in_=h[:, i, :], func=AF.Square,
                accum_out=ss[:, i:i + 1])
        rsqrt_inplace(ss[:, :], "ss", 1.0 / D, EPS)
        for i in range(1, NT):
            s_ = i % 2
            DVE("scalar_tensor_tensor", [f"h:{i}", "ss", "grow"], [f"xs:{s_}"], out=xs[:, s_, :], in0=h[:, i, :],
                scalar=ss[:, i:i + 1], in1=grow, op0=ALU.mult, op1=ALU.mult)
            P.dma("sp", ov[:, i - 1, :], xs[:, s_, :], f"D_out{s_}", reads=[f"xs:{s_}"])
        P.wait_sem("sp", "D_out0")
        P.wait_sem("sp", "D_out1")
    else:
        for i in range(1, NT):
            P.dma("sp", ov[:, i - 1, :], h[:, i, :], "D_out", reads=[f"h:{i}"])
        P.wait_sem("sp", "D_out")

    if dbg:
        print('op counts', P.cnt, {k_: v_ for k_, v_ in P.dcnt.items() if v_ > 2000})
    P.emit()
    es.close()
    return nc


_NC_CACHE = {}


def kernel(**inputs):
    nc = _NC_CACHE.get("nc")
    if nc is None:
        nc = build()
        _NC_CACHE["nc"] = nc
    x = np.ascontiguousarray(inputs["x"], dtype=np.float32)
    in_maps = []
    for c in range(NCORES):
        m = {"x": x[c]}
        for kname, v in inputs.items():
            if kname != "x":
                m[kname] = np.ascontiguousarray(v, dtype=np.float32)
        in_maps.append(m)
    res = run_bass_kernel_spmd(nc, in_maps, core_ids=list(range(NCORES)))
    return np.stack([r["out"] for r in res.results], axis=0)
```
